# Optimizing a Trainium2 kernel written in Bass

```python
import jax, jax.numpy as jnp
from jax import lax
import numpy as np

D_MODEL = 2048
BATCH = 4
SEQ = 2048
DEPTH = 1
DEC_BATCH = 16
DEC_SEQ = 16
PAST_LEN = 2048

CHUNK = 64
N_HEADS = 8
HEAD_DIM = 128
D_ATTN = N_HEADS * HEAD_DIM
D_CONV = 1024
CONV_W = 3
D_FF = 5632
Q_BLOCK = 128
EPS = 1e-6
FFN_SCALE = 0.5
SPLIT_POINTS = (D_CONV, 2 * D_CONV, 3 * D_CONV,
                3 * D_CONV + D_ATTN, 3 * D_CONV + 2 * D_ATTN, 3 * D_CONV + 3 * D_ATTN,
                3 * D_CONV + 3 * D_ATTN + D_MODEL)
D_IN = 3 * D_CONV + 3 * D_ATTN + 2 * D_MODEL

kernel_name = "hybrid_shortconv_stickbreaking_streaming_step"


def _rmsnorm(x, g):
    xf = x.astype(jnp.float32)
    y = xf * lax.rsqrt(jnp.mean(xf * xf, axis=-1, keepdims=True) + EPS)
    return (y * g.astype(jnp.float32)).astype(x.dtype)


def _swiglu(x, w_gate_up, w_down):
    gu = jnp.einsum("btd,df->btf", x, w_gate_up)
    g, u = jnp.split(gu, 2, axis=-1)
    return jnp.einsum("btf,fd->btd", jax.nn.silu(g) * u, w_down)


def _causal_conv(u, hist, w):
    T = u.shape[1]
    full = jnp.concatenate([hist.astype(u.dtype), u], axis=1)
    y = full[:, 0:T] * w[0]
    for i in range(1, CONV_W):
        y = y + full[:, i:i + T] * w[i]
    return y, full[:, -(CONV_W - 1):]


def _stick_breaking_block(q, k, v, q_start):
    Tq, Tk = q.shape[1], k.shape[1]
    z = jnp.einsum("bqhd,bkhd->bhqk", q, k,
                   preferred_element_type=jnp.float32) * (HEAD_DIM ** -0.5)
    t_pos = q_start + jnp.arange(Tq)[:, None]
    s_pos = jnp.arange(Tk)[None, :]
    mask = s_pos < t_pos
    sp = jnp.where(mask, jax.nn.softplus(z), 0.0)
    tail = lax.cumsum(sp, axis=3, reverse=True) - sp
    log_a = jnp.where(mask, jax.nn.log_sigmoid(z) - tail, -jnp.inf)
    a = jnp.exp(log_a)
    return jnp.einsum("bhqk,bkhd->bqhd", a.astype(v.dtype), v)


def _stick_breaking(q, k_all, v_all, q_start):
    T = q.shape[1]
    n_blocks = (T + Q_BLOCK - 1) // Q_BLOCK
    outs = []
    for i in range(n_blocks):
        lo, hi = i * Q_BLOCK, min((i + 1) * Q_BLOCK, T)
        end = q_start + hi
        outs.append(_stick_breaking_block(q[:, lo:hi], k_all[:, :end], v_all[:, :end], q_start + lo))
    return jnp.concatenate(outs, axis=1)


def _layer(x, conv_hist, past_k, past_v, q_start,
           g_ffn1, w_gu1, w_dn1, g_mix, w_in, w_conv, w_conv_out, w_attn_out, w_o,
           g_ffn2, w_gu2, w_dn2):
    B, T, _ = x.shape
    x = x + FFN_SCALE * _swiglu(_rmsnorm(x, g_ffn1), w_gu1, w_dn1)
    h = _rmsnorm(x, g_mix)
    proj = jnp.einsum("btd,de->bte", h, w_in)
    cb, cc, cx, q, k, v, ga, gb = jnp.split(proj, SPLIT_POINTS, axis=-1)
    conv_out, new_hist = _causal_conv(cc * cx, conv_hist, w_conv)
    y_a = jnp.einsum("btc,cd->btd", cb * conv_out, w_conv_out)
    q = q.reshape(B, T, N_HEADS, HEAD_DIM)
    k = k.reshape(B, T, N_HEADS, HEAD_DIM)
    v = v.reshape(B, T, N_HEADS, HEAD_DIM)
    if past_k is None:
        k_all, v_all = k, v
    else:
        k_all = jnp.concatenate([past_k.astype(k.dtype), k], axis=1)
        v_all = jnp.concatenate([past_v.astype(v.dtype), v], axis=1)
    o = _stick_breaking(q, k_all, v_all, q_start).reshape(B, T, D_ATTN)
    y_b = jnp.einsum("bte,ed->btd", o, w_attn_out)
    mixed = jax.nn.sigmoid(ga) * y_a + jax.nn.sigmoid(gb) * y_b
    x = x + jnp.einsum("btd,de->bte", mixed, w_o)
    x = x + FFN_SCALE * _swiglu(_rmsnorm(x, g_ffn2), w_gu2, w_dn2)
    return x, new_hist, k, v


def setup_inputs(seed: int = 0) -> dict:
    key = jax.random.key(seed)
    ks = jax.random.split(key, 20)
    f32 = jnp.float32

    def nrm(k, shape, fan_in):
        return jax.random.normal(k, shape, f32) * (fan_in ** -0.5)

    def gain(k, shape):
        return 1.0 + 0.02 * jax.random.normal(k, shape, f32)

    return {
        "x_prompt": jax.random.normal(ks[0], (BATCH, SEQ, D_MODEL), f32),
        "x_sample": jax.random.normal(ks[1], (DEC_BATCH, DEC_SEQ, D_MODEL), f32),
        "cache_k": jax.random.normal(ks[2], (DEPTH, DEC_BATCH, PAST_LEN, N_HEADS, HEAD_DIM), f32),
        "cache_v": jax.random.normal(ks[3], (DEPTH, DEC_BATCH, PAST_LEN, N_HEADS, HEAD_DIM), f32),
        "state_conv": jax.random.normal(ks[4], (DEPTH, DEC_BATCH, CONV_W - 1, D_CONV), f32),
        "norm_ffn1": gain(ks[5], (DEPTH, D_MODEL)),
        "ffn1_w_gate_up": nrm(ks[6], (DEPTH, D_MODEL, 2 * D_FF), D_MODEL),
        "ffn1_w_down": nrm(ks[7], (DEPTH, D_FF, D_MODEL), D_FF),
        "norm_mix": gain(ks[8], (DEPTH, D_MODEL)),
        "w_in": nrm(ks[9], (DEPTH, D_MODEL, D_IN), D_MODEL),
        "conv_w": nrm(ks[10], (DEPTH, CONV_W, D_CONV), CONV_W),
        "w_conv_out": nrm(ks[11], (DEPTH, D_CONV, D_MODEL), D_CONV),
        "w_attn_out": nrm(ks[12], (DEPTH, D_ATTN, D_MODEL), D_ATTN),
        "w_o": nrm(ks[13], (DEPTH, D_MODEL, D_MODEL), D_MODEL),
        "norm_ffn2": gain(ks[14], (DEPTH, D_MODEL)),
        "ffn2_w_gate_up": nrm(ks[15], (DEPTH, D_MODEL, 2 * D_FF), D_MODEL),
        "ffn2_w_down": nrm(ks[16], (DEPTH, D_FF, D_MODEL), D_FF),
        "norm_final": gain(ks[17], (D_MODEL,)),
    }


def reference(x_prompt, x_sample, cache_k, cache_v, state_conv,
              norm_ffn1, ffn1_w_gate_up, ffn1_w_down, norm_mix, w_in, conv_w,
              w_conv_out, w_attn_out, w_o, norm_ffn2, ffn2_w_gate_up, ffn2_w_down,
              norm_final):
    xp, xs = x_prompt, x_sample
    kp_l, vp_l, cp_l, ks_l, vs_l, cs_l = [], [], [], [], [], []
    for l in range(DEPTH):
        w = (norm_ffn1[l], ffn1_w_gate_up[l], ffn1_w_down[l], norm_mix[l], w_in[l], conv_w[l],
             w_conv_out[l], w_attn_out[l], w_o[l], norm_ffn2[l], ffn2_w_gate_up[l], ffn2_w_down[l])
        hist0 = jnp.zeros((xp.shape[0], CONV_W - 1, D_CONV), xp.dtype)
        xp, c_p, k_p, v_p = _layer(xp, hist0, None, None, 0, *w)
        xs, c_s, k_s, v_s = _layer(xs, state_conv[l], cache_k[l], cache_v[l], PAST_LEN, *w)
        kp_l.append(k_p); vp_l.append(v_p); cp_l.append(c_p)
        ks_l.append(k_s); vs_l.append(v_s); cs_l.append(c_s)
    y_prompt = _rmsnorm(xp, norm_final)
    y_sample = _rmsnorm(xs, norm_final)
    k_prompt = jnp.stack(kp_l, axis=0)
    v_prompt = jnp.stack(vp_l, axis=0)
    conv_prompt = jnp.stack(cp_l, axis=0)
    k_sample = jnp.stack(ks_l, axis=0)
    v_sample = jnp.stack(vs_l, axis=0)
    conv_sample = jnp.stack(cs_l, axis=0)
    return (y_prompt, y_sample, k_prompt, v_prompt, conv_prompt, k_sample, v_sample, conv_sample)
```

```python
import os
import numpy as np
from contextlib import ExitStack
import concourse.bass as bass
import concourse.mybir as mybir
from concourse.bass_utils import run_bass_kernel_spmd

F32 = mybir.dt.float32
BF16 = mybir.dt.bfloat16
AF = mybir.ActivationFunctionType
ALU = mybir.AluOpType

D = 2048
DFF = 5632
DIN = 10240
NH = 8
NT = 1058
CG = [(0, 353), (353, 706), (706, 1058)]
PCOL = 2
SCOL = 1026
EPS = 1e-6
SCALE = 128 ** -0.5
NSLOT = 5
NEG = -30000.0
KSTOP = int(os.environ.get("KSTOP", "99"))
KCORES = int(os.environ.get("KCORES", "8"))


DBG = {}


class _Stop(Exception):
    pass


def _stop(n):
    if KSTOP <= n:
        raise _Stop()


class Buf:
    __slots__ = ("name", "w", "r")

    def __init__(self, name):
        self.name = name
        self.w = None
        self.r = []


class Op:
    __slots__ = ("eng", "fn", "deps", "dma", "group", "gidx", "marked", "rank", "ndep", "cc", "bar", "epoch")

    def __init__(self, eng, fn, dma=False, group=None, cc=False):
        self.eng = eng
        self.fn = fn
        self.deps = []
        self.dma = dma
        self.group = group
        self.gidx = 0
        self.marked = False
        self.rank = 0
        self.ndep = 0
        self.cc = cc
        self.bar = False
        self.epoch = 0


class Sched:
    STRICT = ("act", "dve", "pool")

    def __init__(self, nc, es):
        self.nc = nc
        self.es = es
        self.ops = []
        self.h = {"pe": nc.tensor, "act": nc.scalar, "dve": nc.vector, "pool": nc.gpsimd, "sp": nc.sync}
        self.groups = {}
        self.last = {}
        self.since_barrier = []

    def add(self, eng, fn, reads=(), writes=(), dma=False, group=None, mode="seq", pos=None, cc=False):
        op = Op(eng, fn, dma=dma, group=group, cc=cc)
        deps = []
        for b in reads:
            if b.w is not None:
                deps.append((b.w, True))
        for b in writes:
            if b.w is not None:
                if not (dma and b.w.dma and b not in reads):
                    deps.append((b.w, False))
            for r in b.r:
                deps.append((r, False))
        seen = set()
        for d, raw in deps:
            if d is op or id(d) in seen:
                continue
            if not d.dma and d.eng == eng and not dma:
                if eng not in self.STRICT:
                    continue
            if not d.dma and d.eng == eng and dma:
                continue
            seen.add(id(d))
            op.deps.append(d)
            d.ndep += 1
        if dma:
            for b in list(reads) + list(writes):
                cands = ([b.w] if b.w is not None else []) + (list(b.r) if b in writes else [])
                for d in cands:
                    if d is not None and not d.dma and d.eng == eng and id(d) not in seen:
                        seen.add(id(d))
                        op.deps.append(d)
                        d.ndep += 1
        for b in reads:
            b.r.append(op)
        for b in writes:
            b.w = op
            b.r = []
        if dma or cc:
            g = self.groups.setdefault(group, [mode, 0])
            g[1] += 1
            op.gidx = g[1]
        if pos is None:
            self.ops.append(op)
        else:
            self.ops.insert(pos, op)
        if not dma and not cc:
            self.last[eng] = op
        self.since_barrier.append(op)
        return op

    def index_after(self, op):
        return self.ops.index(op) + 1

    def barrier(self, skip_pool=False):
        lasts = dict(self.last)
        if skip_pool:
            lasts.pop("pool", None)
        lastg = {}
        for o in self.since_barrier:
            if o.dma:
                lastg[o.group] = o
        pend = list(lastg.values())
        for e in ("pe", "act", "dve", "pool", "sp"):
            op = Op(e, None)
            op.bar = True
            for e2, l in lasts.items():
                if e2 != e:
                    op.deps.append(l)
                    l.ndep += 1
            for p in pend:
                op.deps.append(p)
                p.ndep += 1
            self.ops.append(op)
        self.since_barrier = []

    def emit(self):
        nc, es = self.nc, self.es
        ep = 0
        prev_bar = False
        for op in self.ops:
            if prev_bar and not op.bar:
                ep += 1
            op.epoch = ep
            prev_bar = op.bar
        nep = ep + 1
        esem = {(e, k): es.enter_context(nc.semaphore(f"e_{e}_{k}")) for e in self.h for k in range(nep)}
        gsem = {g: es.enter_context(nc.semaphore("g_" + g)) for g in self.groups}
        for op in self.ops:
            for d in op.deps:
                if not d.dma and not d.cc and (d.epoch == op.epoch or d.eng == "pool"):
                    d.marked = True
        cnt = {(e, k): 0 for e in self.h for k in range(nep)}
        for op in self.ops:
            if op.marked:
                cnt[(op.eng, op.epoch)] += 1
                op.rank = cnt[(op.eng, op.epoch)]
        seen = {e: {} for e in self.h}
        for op in self.ops:
            need = {}
            for d in op.deps:
                if d.cc:
                    key, val = ("g", d.group), d.gidx
                elif d.dma:
                    mode, total = self.groups[d.group]
                    key, val = ("g", d.group), 16 * (total if mode == "all" else d.gidx)
                else:
                    if d.epoch != op.epoch and d.eng != "pool":
                        assert d.epoch < op.epoch
                        continue
                    key, val = ("e", (d.eng, d.epoch)), d.rank
                if need.get(key, 0) < val:
                    need[key] = val
            hnd = self.h[op.eng]
            for key, val in need.items():
                if seen[op.eng].get(key, 0) >= val:
                    continue
                seen[op.eng][key] = val
                sem = gsem[key[1]] if key[0] == "g" else esem[key[1]]
                hnd.wait_ge(sem, val)
            if op.fn is None:
                continue
            ins = op.fn()
            if op.cc:
                ins.then_inc(gsem[op.group])
            elif op.dma:
                ins.then_inc(gsem[op.group], 16)
            elif op.marked:
                ins.then_inc(esem[(op.eng, op.epoch)], 1)
        sp = self.h["sp"]
        self.final_counts = dict(cnt)
        for g, (mode, total) in self.groups.items():
            if not g.startswith("cc"):
                sp.wait_ge(gsem[g], 16 * total)
        for (e, k), c in cnt.items():
            if c and k == nep - 1:
                sp.wait_ge(esem[(e, k)], c)


def build_program():
    nc = bass.Bass("TRN2", target_bir_lowering=False)
    es = ExitStack()
    S = Sched(nc, es)

    def din(name, shape, dt=F32):
        return nc.dram_tensor(name, list(shape), dt, kind="ExternalInput").ap()

    def dout(name, shape, dt=F32):
        return nc.dram_tensor(name, list(shape), dt, kind="ExternalOutput").ap()

    xT = din("xT", [D, NT])
    w_gu1 = din("w_gu1", [D, 2 * DFF]); w_dn1 = din("w_dn1", [DFF, D])
    w_in = din("w_in", [D, DIN])
    w_co = din("w_co", [1024, D]); w_ao = din("w_ao", [1024, D]); w_o = din("w_o", [D, D])
    w_gu2 = din("w_gu2", [D, 2 * DFF]); w_dn2 = din("w_dn2", [DFF, D])
    gains = din("gains", [128, 4 * 16])
    convw = din("convw", [128, 8 * 3])
    stateT = din("stateT", [128, 8 * 4])
    flags = din("flags", [128, 1])
    ckT = din("ckT", [2, 8, 128, NH * 256])
    cv = din("cv", [2, 2048, 1024])
    yT = dout("yT", [D, NT]); kTo = dout("kTo", [1024, NT]); vTo = dout("vTo", [1024, NT])
    convo = dout("convo", [128, 8 * 6])
    srcs = [[nc.dram_tensor(f"kv_src{kv}_{hp}", [256, 1024], BF16) for hp in range(4)] for kv in range(2)]
    gats = [[nc.dram_tensor(f"kv_gat{kv}_{hp}", [512, 1024], BF16) for hp in range(4)] for kv in range(2)]

    uniq = [0]

    def sb(name, shape, dt, ctx=None):
        uniq[0] += 1
        DBG[name] = f"{name}_{uniq[0]}"
        return (ctx or es).enter_context(nc.sbuf_tensor(f"{name}_{uniq[0]}", list(shape), dt))

    xres = sb("xres", [128, 16, NT], F32)
    ring = [sb(f"ring{i}", [128, 4096], BF16) for i in range(NSLOT)]
    gain_t = sb("gain_t", [128, 64], F32)
    convw_t = sb("convw_t", [128, 24], F32)
    state_t = sb("state_t", [128, 32], F32)
    flag_t = sb("flag_t", [128, 1], F32)
    iota_t = sb("iota_t", [128, 128], F32)
    tri = sb("tri", [128, 128], BF16)
    utri = sb("utri", [128, 128], BF16)
    ident = sb("ident", [128, 128], BF16)
    maskf = sb("maskf", [128, 128], F32)
    mask8 = sb("mask8", [16, 128], F32)
    ones_b = sb("ones_b", [128, 128], BF16)
    convo_t = sb("convo_t", [128, 48], F32)
    relay_t = sb("relay_t", [128, 8], F32)

    banks = [es.enter_context(nc.psum_tensor(f"bank{i}", [128, 512], F32)) for i in range(8)]
    bank_buf = [Buf(f"bank{i}") for i in range(8)]
    held = set()
    ps_state = {"i": 0}

    def ps_alloc(hold=False):
        for _ in range(16):
            i = ps_state["i"] % 8
            ps_state["i"] += 1
            if i not in held:
                if hold:
                    held.add(i)
                return i
        raise RuntimeError("no psum bank")

    def ps_release(i):
        held.discard(i)

    def mm(out, lhsT, rhs, reads, writes, start=True, stop=True, skip=False):
        if skip:
            return S.add("pe", lambda: nc.tensor.matmul(out, lhsT, rhs, start=start, stop=stop, skip_group_check=True),
                         reads, writes)
        return S.add("pe", lambda: nc.tensor.matmul(out, lhsT, rhs, start=start, stop=stop), reads, writes)

    def act(out, in_, func, reads, writes, **kw):
        return S.add("act", lambda: nc.scalar.activation(out=out, in_=in_, func=func, **kw), reads, writes)

    def vec(fn, reads, writes, eng="dve"):
        return S.add(eng, fn, reads, writes)

    def dma(eng, out, in_, reads, writes, group, mode="seq", pos=None):
        h = S.h[eng]
        return S.add(eng, lambda: h.dma_start(out=out, in_=in_), reads, writes, dma=True, group=group, mode=mode, pos=pos)

    slot_buf = [Buf(f"slot{i}") for i in range(NSLOT)]
    wst = {"n": 0}

    def wload(dram_view, kc, cols):
        i = wst["n"]
        wst["n"] += 1
        s = i % NSLOT
        b = slot_buf[s]
        view = ring[s][:, 0:kc * cols].rearrange("p (k c) -> p k c", k=kc)
        pos = None
        if b.r:
            pos = S.index_after(b.r[-1])
        elif b.w is not None:
            pos = S.index_after(b.w)
        elif wst.get("early") is not None:
            pos = wst["early"]
            wst["early"] += 1
        dma("pool", view, dram_view, [], [b], group=f"slot{s}", pos=pos)
        return view, b

    def wpanel(w, r0, nr, c0, ncol):
        return w[r0:r0 + nr, c0:c0 + ncol].rearrange("(k p) c -> p k c", p=128)

    CST = Buf("const")
    for t, d_ in ((gain_t, gains), (convw_t, convw), (state_t, stateT), (flag_t, flags)):
        dma("sp", t[:], d_, [], [CST], group="const", mode="all")
    XB = [[Buf(f"x{c}_{g}") for g in range(3)] for c in range(16)]
    for g, (a, b_) in enumerate(CG):
        dma("sp", xres[:, :, a:b_], xT[:, a:b_].rearrange("(c p) n -> p c n", p=128), [],
            [XB[c][g] for c in range(16)], group=f"xload{g}")
    wst["early"] = len(S.ops)
    IOT = Buf("iota")
    S.add("pool", lambda: nc.gpsimd.iota(iota_t[:], pattern=[[1, 128]], base=0, channel_multiplier=-1,
                                         allow_small_or_imprecise_dtypes=True), [], [IOT])
    K = Buf("konst")
    vec(lambda: nc.vector.tensor_single_scalar(out=tri[:], in_=iota_t[:], scalar=0.0, op=ALU.is_le), [IOT], [K])
    vec(lambda: nc.vector.tensor_single_scalar(out=utri[:], in_=iota_t[:], scalar=0.0, op=ALU.is_gt), [IOT], [K])
    vec(lambda: nc.vector.tensor_single_scalar(out=ident[:], in_=iota_t[:], scalar=0.0, op=ALU.is_equal), [IOT], [K])
    vec(lambda: nc.vector.tensor_single_scalar(out=maskf[:], in_=iota_t[:], scalar=0.0, op=ALU.is_gt), [IOT], [K])
    K2 = Buf("konst2")
    for h in range(NH):
        vec(lambda h=h: nc.vector.tensor_single_scalar(out=mask8[:, h * 16:(h + 1) * 16], in_=iota_t[0:16, 0:16],
                                                       scalar=0.0, op=ALU.is_gt), [IOT], [K2])
    vec(lambda: nc.vector.memset(ones_b[:], 1.0), [], [K])

    def norm(gidx, xn, XN, scr, final_out=None, per_cg=False):
        sq, rs, rstd = scr["sq"], scr["rs"], scr["rstd"]
        SQ, RS, RSTD = scr["SQ"], scr["RS"], scr["RSTD"]
        bks = [ps_alloc(hold=True) for _ in range(3)]
        kq = [0]

        def squares(g):
            a, b_ = CG[g]; n = b_ - a
            for c in range(16):
                s = kq[0] % len(sq); kq[0] += 1
                act(sq[s][:, 0:n], xres[:, c, a:b_], AF.Square, [XB[c][g]], [SQ[s]])
                mm(banks[bks[g]][:, 0:n], ones_b[:], sq[s][:, 0:n], [SQ[s], K], [bank_buf[bks[g]]], start=(c == 0), stop=(c == 15))

        def stats(g):
            a, b_ = CG[g]; n = b_ - a
            act(rs[g][:, 0:n], banks[bks[g]][:, 0:n], AF.Sqrt, [bank_buf[bks[g]], scr["EPSB"]], [RS[g]], scale=1.0 / D,
                bias=scr["eps"][:, 0:1])
            vec(lambda n=n, g=g: nc.vector.reciprocal(out=rstd[g][:, 0:n], in_=rs[g][:, 0:n]), [RS[g]], [RSTD[g]])

        def scaling(g):
            a, b_ = CG[g]; n = b_ - a
            for c in range(16):
                gcol = gain_t[:, gidx * 16 + c: gidx * 16 + c + 1]
                st, ST = final_out if final_out is not None else (None, None)
                if final_out is not None:
                    s_ = (g * 16 + c) % len(st)
                    o_ap, OBUF = st[s_][:, 0:n], ST[s_]
                else:
                    o_ap, OBUF = xn[:, c, a:b_], XN[c][g]
                if c % 3 != 2:
                    vec(lambda c=c, a=a, b_=b_, n=n, g=g, gcol=gcol, o_ap=o_ap: nc.vector.scalar_tensor_tensor(
                        out=o_ap, in0=xres[:, c, a:b_], scalar=gcol, in1=rstd[g][:, 0:n],
                        op0=ALU.mult, op1=ALU.mult), [XB[c][g], RSTD[g], CST], [OBUF])
                else:
                    t_ = scr["nk"][0] % 2; scr["nk"][0] += 1
                    act(scr["nt"][t_][:, 0:n], xres[:, c, a:b_], AF.Copy, [XB[c][g], CST], [scr["NT"][t_]], scale=gcol)
                    S.add("pool", lambda t_=t_, n=n, g=g, o_ap=o_ap: nc.gpsimd.tensor_tensor(
                        out=o_ap, in0=scr["nt"][t_][:, 0:n], in1=rstd[g][:, 0:n], op=ALU.mult),
                        [scr["NT"][t_], RSTD[g]], [OBUF])
                if final_out is not None:
                    dma("sp", yT[c * 128:(c + 1) * 128, a:b_], o_ap, [OBUF], [], group=f"yout{s_}")

        if per_cg:
            squares(0)
            for g in range(3):
                stats(g)
                if g + 1 < 3:
                    squares(g + 1)
                scaling(g)
        else:
            for g in range(3):
                squares(g)
            for g in range(3):
                stats(g)
            for g in range(3):
                scaling(g)
        for b in bks:
            ps_release(b)

    def norm_scratch(ctx):
        scr = {
            "sq": [sb(f"sq{i}", [128, 353], BF16, ctx) for i in range(4)],
            "rs": [sb(f"rs{i}", [128, 353], F32, ctx) for i in range(3)],
            "rstd": [sb(f"rstd{i}", [128, 353], F32, ctx) for i in range(3)],
            "eps": sb("epsc", [128, 1], F32, ctx),
            "nt": [sb(f"nt{i}", [128, 353], F32, ctx) for i in range(2)], "NT": [Buf("nt0"), Buf("nt1")], "nk": [0],
            "SQ": [Buf(f"sq{i}") for i in range(4)], "RS": [Buf(f"rs{i}") for i in range(3)], "RSTD": [Buf(f"rstd{i}") for i in range(3)],
        }
        scr["EPSB"] = Buf("epsb")
        vec(lambda: nc.vector.memset(scr["eps"][:], EPS), [], [scr["EPSB"]])
        return scr

    def ffn(xn, XN, w_gu, w_dn, ctx):
        hb = [sb(f"hb{i}", [128, 4, NT], BF16, ctx) for i in range(2)]
        HB = [[[Buf(f"hb{i}_{j}_{g}") for g in range(3)] for j in range(4)] for i in range(2)]
        tmp = [sb(f"stmp{i}", [128, 353], F32, ctx) for i in range(2)]
        TMP = [Buf("stmp0"), Buf("stmp1")]
        tcount = [0]
        NG = DFF // 512

        def phaseA(gi):
            s = gi % 2
            for half in range(2):
                c0 = gi * 512 + half * 256
                gv, gb_ = wload(wpanel(w_gu, 0, D, c0, 256), 16, 256)
                uv, ub_ = wload(wpanel(w_gu, 0, D, DFF + c0, 256), 16, 256)
                for jj in range(2):
                    j = half * 2 + jj
                    for g, (a, b_) in enumerate(CG):
                        n = b_ - a
                        bg = ps_alloc(); bu = ps_alloc()
                        for kc in range(16):
                            mm(banks[bg][:, 0:n], gv[:, kc, jj * 128:(jj + 1) * 128], xn[:, kc, a:b_],
                               [gb_, XN[kc][g]], [bank_buf[bg]], start=(kc == 0), stop=(kc == 15))
                        for kc in range(16):
                            mm(banks[bu][:, 0:n], uv[:, kc, jj * 128:(jj + 1) * 128], xn[:, kc, a:b_],
                               [ub_, XN[kc][g]], [bank_buf[bu]], start=(kc == 0), stop=(kc == 15))
                        t = tcount[0] % 2; tcount[0] += 1
                        act(tmp[t][:, 0:n], banks[bg][:, 0:n], AF.Silu, [bank_buf[bg]], [TMP[t]])
                        vec(lambda s=s, j=j, a=a, b_=b_, n=n, t=t, bu=bu: nc.vector.tensor_tensor(
                            out=hb[s][:, j, a:b_], in0=tmp[t][:, 0:n], in1=banks[bu][:, 0:n], op=ALU.mult),
                            [TMP[t], bank_buf[bu]], [HB[s][j][g]])

        def phaseB(gi):
            s = gi % 2
            for half in range(2):
                dv, db_ = wload(wpanel(w_dn, gi * 512, 512, half * 1024, 1024), 4, 1024)
                for mo in range(8):
                    m = half * 8 + mo
                    for g, (a, b_) in enumerate(CG):
                        n = b_ - a
                        bk = ps_alloc()
                        for kc in range(4):
                            mm(banks[bk][:, 0:n], dv[:, kc, mo * 128:(mo + 1) * 128], hb[s][:, kc, a:b_],
                               [db_, HB[s][kc][g]], [bank_buf[bk]], start=(kc == 0), stop=(kc == 3))
                        vec(lambda m=m, a=a, b_=b_, n=n, bk=bk: nc.vector.scalar_tensor_tensor(
                            out=xres[:, m, a:b_], in0=banks[bk][:, 0:n], scalar=0.5, in1=xres[:, m, a:b_],
                            op0=ALU.mult, op1=ALU.add), [bank_buf[bk], XB[m][g]], [XB[m][g]])

        for gi in range(NG):
            phaseA(gi)
            if gi >= 1:
                phaseB(gi - 1)
        phaseB(NG - 1)

    try:
        _build_phases(locals())
    except _Stop:
        pass
    S.emit()
    if os.environ.get("KDEBUG"):
        print("marked counts", S.final_counts, "groups", {g: v[1] for g, v in S.groups.items()})
    es.close()
    return nc


def _build_phases(L):
    globals_ = globals()
    for k_, v_ in L.items():
        globals_[k_] = v_
    with ExitStack() as ph:
        xn = sb("xn", [128, 16, NT], BF16, ph)
        XN = [[Buf(f"xn{c}_{g}") for g in range(3)] for c in range(16)]
        scr = norm_scratch(ph)
        norm(0, xn, XN, scr, per_cg=True)
        if KSTOP <= 0:
            return
        ffn(xn, XN, w_gu1, w_dn1, ph)
        S.barrier()
    if KSTOP <= 1:
        return

    with ExitStack() as ph2:
        oT = sb("oT", [128, NH, NT], BF16, ph2)
        ksamp = sb("ksamp", [128, NH, 32], BF16, ph2)
        vsamp = sb("vsamp", [128, NH, 32], BF16, ph2)
        pq = ph2.enter_context(ExitStack())
        qT = sb("qT", [128, NH, NT], BF16, pq)
        QB = [Buf(f"q{h}") for h in range(NH)]
        OB = [Buf(f"o{h}") for h in range(NH)]
        KS = Buf("ksamp"); VS = Buf("vsamp")
        vec(lambda: nc.vector.memset(oT[:], 0.0), [], OB)

        SRCS = [Buf("kvsrc0"), Buf("kvsrc1")]
        GATS = [[Buf(f"gat{kv}_{hp}") for hp in range(4)] for kv in range(2)]
        RELAY = [[Buf(f"relay{kv}_{hp}") for hp in range(4)] for kv in range(2)]
        SRCB = [[Buf(f"srcb{kv}_{h}") for h in range(NH)] for kv in range(2)]
        with ExitStack() as pa:
            xn = sb("xn", [128, 16, NT], BF16, pa)
            XN = [[Buf(f"xn{c}_{g}") for g in range(3)] for c in range(16)]
            scr = norm_scratch(pa)
            norm(1, xn, XN, scr)
            st32 = [sb(f"st32_{i}", [128, NT], F32, pa) for i in range(2)]
            st16 = [sb(f"st16_{i}", [128, NT], BF16, pa) for i in range(2)]
            ST32 = [Buf("st32_0"), Buf("st32_1")]; ST16 = [Buf("st16_0"), Buf("st16_1")]
            sc = [0]
            for kind in range(int(os.environ.get("KKINDS", "3"))):
                for pn in range(4):
                    c0 = 3072 + kind * 1024 + pn * 256
                    wv, wb = wload(wpanel(w_in, 0, D, c0, 256), 16, 256)
                    for jj in range(2):
                        h = pn * 2 + jj
                        s = sc[0] % 2; sc[0] += 1
                        for g, (a, b_) in enumerate(CG):
                            n = b_ - a
                            bk = ps_alloc()
                            for kc in range(16):
                                mm(banks[bk][:, 0:n], wv[:, kc, jj * 128:(jj + 1) * 128], xn[:, kc, a:b_],
                                   [wb, XN[kc][g]], [bank_buf[bk]], start=(kc == 0), stop=(kc == 15))
                            if kind == 0:
                                act(qT[:, h, a:b_], banks[bk][:, 0:n], AF.Copy, [bank_buf[bk]], [QB[h]])
                            else:
                                act(st32[s][:, a:b_], banks[bk][:, 0:n], AF.Copy, [bank_buf[bk]], [ST32[s]])
                                vec(lambda s=s, a=a, b_=b_: nc.vector.tensor_copy(
                                    out=st16[s][:, a:b_], in_=st32[s][:, a:b_]), [ST32[s]], [ST16[s]])
                        if kind > 0:
                            od = kTo if kind == 1 else vTo
                            if not os.environ.get("KNOOUT"):
                                dma("sp", od[h * 128:(h + 1) * 128, :], st32[s][:], [ST32[s]], [], group=f"kvout{s}")
                            r0 = (h % 2) * 128
                            sd = srcs[kind - 1][h // 2]
                            if not os.environ.get("KNOSRC"):
                                dma("sp", sd[r0:r0 + 128, :], st16[s][:, PCOL:PCOL + 1024], [ST16[s]], [SRCB[kind - 1][h]], group=f"kvsrc{s}")
                            dst, DB = (ksamp, KS) if kind == 1 else (vsamp, VS)
                            if not os.environ.get("KNOKS"):
                                vec(lambda s=s, h=h, dst=dst: nc.vector.tensor_copy(out=dst[:, h, :], in_=st16[s][:, SCOL:NT]),
                                    [ST16[s]], [DB])
                            if h % 2 == 1 and not os.environ.get("KNOCC"):
                                kv, hp_ = kind - 1, h // 2
                                S.add("pool", lambda kv=kv, hp_=hp_: nc.gpsimd.collective_compute(
                                    "AllGather", ALU.bypass, replica_groups=[[2 * i, 2 * i + 1] for i in range(KCORES // 2)],
                                    ins=[srcs[kv][hp_].ap().opt()], outs=[gats[kv][hp_].ap().opt()]),
                                    [SRCB[kv][h - 1], SRCB[kv][h]], [GATS[kv][hp_]], group="cc", cc=True)
                                S.add("pool", lambda: nc.gpsimd.memset(relay_t[:], 0.0), [GATS[kv][hp_]], [RELAY[kv][hp_]])
            S.barrier(skip_pool=True)
        if KSTOP <= 2:
            return

        with ExitStack() as pb:
            KB = [sb(f"KB{i}", [128, 4096], BF16, pb) for i in range(2)]
            VB = [sb(f"VB{i}", [128, 4096], BF16, pb) for i in range(2)]
            VT = sb("VTt", [128, 4096], BF16, pb)
            NB = 3
            e_t = [sb(f"e{i}", [128, 512], F32, pb) for i in range(NB)]
            sp_t = [sb(f"sp{i}", [128, 512], BF16, pb) for i in range(NB)]
            w_t = [sb(f"w{i}", [128, 512], F32, pb) for i in range(NB)]
            a_t = [sb(f"a{i}", [128, 512], BF16, pb) for i in range(NB)]
            Eb = [Buf(f"e{i}") for i in range(NB)]; SPb = [Buf(f"sp{i}") for i in range(NB)]
            Wb = [Buf(f"w{i}") for i in range(NB)]; Ab = [Buf(f"a{i}") for i in range(NB)]
            vstok = sb("vstok", [16, NH, 128], BF16, pb)
            vstok2 = sb("vstok2", [16, NH, 128], BF16, pb)
            VSTOK = [Buf("vstok"), Buf("vstok2")]
            zb = sb("zerob", [128, 1], F32, pb)
            ZB = Buf("zb")
            vec(lambda: nc.vector.memset(zb[:], 0.0), [], [ZB])
            blk = [0]

            def run_pipeline(tasks):
                nst = 6
                n = len(tasks)
                for t_ in tasks:
                    t_["ctx"]["i"] = blk[0] % NB
                    blk[0] += 1
                for step in range(n + nst - 1):
                    for s in reversed(range(nst)):
                        t = step - s
                        if 0 <= t < n:
                            tasks[t]["st"][s]()

            def make_block(qk_list, av_list, ncol, kp, bias_ap, mask_fn, bC, bO, c0, extra_reads):
                st = {"i": None}

                def s0():
                    st["bz"] = ps_alloc()
                    bz = st["bz"]
                    for (oc, n_, l, r, rd) in qk_list:
                        mm(banks[bz][0:kp, oc:oc + n_], l, r, rd, [bank_buf[bz]])

                def s1():
                    bz = st["bz"]
                    act(e_t[st["i"]][0:kp, 0:ncol], banks[bz][0:kp, 0:ncol], AF.Exp, [bank_buf[bz], CST, ZB], [Eb[st["i"]]],
                        scale=SCALE, bias=bias_ap)
                    if mask_fn is not None:
                        mask_fn(e_t[st["i"]], Eb[st["i"]])

                def s2():
                    act(sp_t[st["i"]][0:kp, 0:ncol], e_t[st["i"]][0:kp, 0:ncol], AF.Ln, [Eb[st["i"]]], [SPb[st["i"]]], bias=1.0)

                def s3():
                    mm(banks[bC][0:kp, c0:c0 + ncol], tri[0:kp, 0:kp], sp_t[st["i"]][0:kp, 0:ncol], [SPb[st["i"]], K],
                       [bank_buf[bC]], start=False, stop=True, skip=True)

                def s4():
                    act(w_t[st["i"]][0:kp, 0:ncol], banks[bC][0:kp, c0:c0 + ncol], AF.Exp, [bank_buf[bC]], [Wb[st["i"]]], scale=-1.0)
                    vec(lambda: nc.vector.tensor_tensor(out=a_t[st["i"]][0:kp, 0:ncol], in0=e_t[st["i"]][0:kp, 0:ncol],
                                                        in1=w_t[st["i"]][0:kp, 0:ncol], op=ALU.mult), [Eb[st["i"]], Wb[st["i"]]], [Ab[st["i"]]])

                def s5():
                    mm(banks[bC][:, c0:c0 + ncol], utri[0:kp, :], sp_t[st["i"]][0:kp, 0:ncol], [SPb[st["i"]], K],
                       [bank_buf[bC]], start=False, stop=True, skip=True)
                    for (oc, n_, l, ac, rd) in av_list:
                        mm(banks[bO][:, oc:oc + n_], l, a_t[st["i"]][0:kp, ac:ac + n_], [Ab[st["i"]]] + rd + extra_reads,
                           [bank_buf[bO]], start=False, stop=True, skip=True)

                return {"st": [s0, s1, s2, s3, s4, s5], "ctx": st}

            def diag_mask(sub0):
                def f(et, EB):
                    vec(lambda: nc.vector.tensor_tensor(out=et[:, sub0:sub0 + 128], in0=et[:, sub0:sub0 + 128],
                                                        in1=maskf[:], op=ALU.mult), [EB, K], [EB])
                return f

            KBO = [Buf("KBO0"), Buf("KBO1")]; KBT = [Buf("KBT0"), Buf("KBT1")]
            VBO = [Buf("VBO0"), Buf("VBO1")]; VBT = [Buf("VBT0"), Buf("VBT1")]
            VTO = Buf("VTO"); VTT = Buf("VTT")
            KBb = [[KBO[0], KBT[0]], [KBO[1], KBT[1]]]
            VBb = [[VBO[0], VBT[0]], [VBO[1], VBT[1]]]

            def pair_views(hp):
                kb_, vb_ = KB[hp % 2], VB[hp % 2]
                kview = kb_[:].rearrange("p (h k) -> p h k", h=2)
                vview = vb_[:].rearrange("p (h t d) -> p h t d", h=2, t=16)
                return kview, vview

            vtv = VT[:].rearrange("p (h k) -> p h k", h=2)

            def load_pair(hp):
                kview, vview = pair_views(hp)
                sl = hp % 2
                for hh in range(2):
                    rr = slice(hh * 128, (hh + 1) * 128)
                    dma("sp", kview[:, hh, 1024:2048], srcs[0][hp][rr, :], [SRCB[0][2 * hp + hh]], [KBO[sl]], group=f"kbo{sl}")
                    dma("sp", vtv[:, hh, 1024:2048], srcs[1][hp][rr, :], [SRCB[1][2 * hp + hh]], [VTO], group="vto")
                for hh in range(2):
                    rr = slice(hh * 128, (hh + 1) * 128)
                    dma("sp", kview[:, hh, 0:1024], gats[0][hp][rr, :], [RELAY[0][hp]], [KBT[sl]], group=f"kbt{sl}")
                    dma("sp", vtv[:, hh, 0:1024], gats[1][hp][rr, :], [RELAY[1][hp]], [VTT], group="vtt")

            def transpose_pair(hp):
                kview, vview = pair_views(hp)
                sl = hp % 2
                for t4 in (2, 3, 0, 1):
                    src_b, dst_b = (VTO, VBO[sl]) if t4 >= 2 else (VTT, VBT[sl])
                    for hh in range(2):
                        bk = ps_alloc()
                        pv = banks[bk][:].bitcast(BF16)
                        for tt in range(4):
                            t = t4 * 4 + tt
                            S.add("pe", lambda hh=hh, t=t, tt=tt, pv=pv: nc.tensor.transpose(
                                out=pv[:, tt * 128:(tt + 1) * 128], in_=vtv[:, hh, t * 128:(t + 1) * 128], identity=ident[:]),
                                [src_b, K], [bank_buf[bk]])
                        vec(lambda hh=hh, t4=t4, pv=pv, vview=vview: nc.vector.tensor_copy(
                            out=vview[:, hh, t4 * 4:(t4 + 1) * 4, :], in_=pv[:, 0:512].rearrange("p (t d) -> p t d", t=4)),
                            [bank_buf[bk]], [dst_b])

            sbufs = [[KB[0], KB[1]], [VB[0], VB[1]]]
            SBb = [[KBb[0], KBb[1]], [VBb[0], VBb[1]]]

            def cache_views(sbi, k_):
                t_ = sbufs[sbi][k_ % 2]
                kv_ = t_[:, 0:2048].rearrange("p (h k) -> p h k", h=NH)
                vv_ = t_[:, 2048:4096].rearrange("p (t d) -> p t d", t=2)
                return kv_, vv_, SBb[sbi][k_ % 2]

            def issue_panel(k_):
                pn = 7 - k_
                for sbi in range(2):
                    kv_, vv_, bb = cache_views(sbi, k_)
                    dma("pool", kv_, ckT[sbi, pn, :, :].rearrange("d (h k) -> d h k", h=NH), [], bb,
                        group=f"ck{sbi}{k_ % 2}")
                    dma("pool", vv_, cv[sbi, pn * 256:(pn + 1) * 256, :].rearrange("(t p) d -> p t d", p=128), [], bb,
                        group=f"ck{sbi}{k_ % 2}")

            noop = lambda: None

            def pseudo(f0=noop, f4=noop):
                return {"st": [f0, noop, noop, noop, f4, noop], "ctx": {}}

            bCs = [ps_alloc(hold=True) for _ in range(2)]
            bOs = [ps_alloc(hold=True) for _ in range(2)]

            def evac_and_clear(prev):
                def f():
                    if prev is not None:
                        php, pqt = prev
                        pq0 = PCOL + pqt * 512
                        for hh in range(2):
                            h = php * 2 + hh
                            vec(lambda h=h, hh=hh, pq0=pq0: nc.vector.tensor_copy(
                                out=oT[:, h, pq0:pq0 + 512], in_=banks[bOs[hh]][:, :]), [bank_buf[bOs[hh]]], [OB[h]])
                    for b in bCs + bOs:
                        vec(lambda b=b: nc.vector.memset(banks[b][:], 0.0), [], [bank_buf[b]])
                return f

            load_pair(0)
            transpose_pair(0)
            all_tasks = []
            prev = None
            for hp in range(4):
                kview, vview = pair_views(hp)
                sl = hp % 2
                for qt in range(2):
                    q0 = PCOL + qt * 512
                    hook0 = noop
                    if qt == 1 and hp + 1 < 4:
                        hook0 = (lambda hp=hp: load_pair(hp + 1))

                    all_tasks.append(pseudo(hook0, evac_and_clear(prev)))
                    all_tasks.append(pseudo())
                    streams = []
                    for hh in range(2):
                        h = hp * 2 + hh
                        tl = []
                        for kb in range(4 * qt + 3, -1, -1):
                            kc0 = 1024 + kb * 128
                            if kb >= 4 * qt:
                                sub = (kb - 4 * qt) * 128
                                ncol = 512 - sub
                                mfn = diag_mask(0)
                            else:
                                sub, ncol, mfn = 0, 512, None
                            qk = [(0, ncol, kview[:, hh, kc0:kc0 + 128], qT[:, h, q0 + sub:q0 + 512], [KBO[sl], QB[h]])]
                            av = [(sub, ncol, vview[:, hh, 8 + kb, :], 0, [VBO[sl]])]
                            tl.append(make_block(qk, av, ncol, 128, zb[:, 0:1], mfn, bCs[hh], bOs[hh], sub, []))
                        for kb in range(7, -1, -1):
                            kc0 = kb * 128
                            qk = [(0, 512, kview[:, hh, kc0:kc0 + 128], qT[:, h, q0:q0 + 512], [KBT[sl], QB[h]])]
                            av = [(0, 512, vview[:, hh, kb, :], 0, [VBT[sl]])]
                            tl.append(make_block(qk, av, 512, 128, flag_t[:, 0:1], None, bCs[hh], bOs[hh], 0, []))
                        streams.append(tl)
                    gtasks = [t for pair in zip(*streams) for t in pair]
                    if qt == 1 and hp + 1 < 4:
                        gtasks.insert(16, pseudo(lambda hp=hp: transpose_pair(hp + 1)))
                    if qt == 0 and hp == 3 and KSTOP > 3:
                        gtasks.insert(16, pseudo(lambda: issue_panel(0)))
                    all_tasks += gtasks
                    prev = (hp, qt)
            run_pipeline(all_tasks)
            evac_and_clear(prev)()
            for b in bCs + bOs:
                ps_release(b)

            if KSTOP <= 3:
                return
            bk = ps_alloc()
            pv = banks[bk][:].bitcast(BF16)
            vstoks = [vstok, vstok2]
            for sbi in range(2):
                if sbi == 1:
                    bk = ps_alloc()
                    pv = banks[bk][:].bitcast(BF16)
                for h in range(NH):
                    S.add("pe", lambda h=h, pv=pv, sbi=sbi: nc.tensor.transpose(
                        out=pv[0:16, h * 128:(h + 1) * 128], in_=vsamp[:, h, sbi * 16:(sbi + 1) * 16], identity=ident[:]),
                        [VS, K], [bank_buf[bk]])
                vec(lambda sbi=sbi, pv=pv: nc.vector.tensor_copy(
                    out=vstoks[sbi][:], in_=pv[0:16, 0:1024].rearrange("p (h d) -> p h d", h=NH)), [bank_buf[bk]], [VSTOK[sbi]])
            bC = ps_alloc(hold=True); bO = ps_alloc(hold=True)
            for b in (bC, bO):
                vec(lambda b=b: nc.vector.memset(banks[b][:], 0.0), [], [bank_buf[b]])

            def run_now(t):
                t["ctx"]["i"] = blk[0] % NB
                blk[0] += 1
                for f in t["st"]:
                    f()
            qcs = [SCOL, SCOL + 16]
            qk = [(sbi * 128 + h * 16, 16, ksamp[:, h, sbi * 16:(sbi + 1) * 16], qT[:, h, qcs[sbi]:qcs[sbi] + 16], [KS, QB[h]])
                  for sbi in range(2) for h in range(NH)]
            av = [(sbi * 128 + h * 16, 16, vstoks[sbi][0:16, h, :], sbi * 128 + h * 16, [VSTOK[sbi]])
                  for sbi in range(2) for h in range(NH)]

            def m16(et, EB):
                for sbi in range(2):
                    vec(lambda sbi=sbi: nc.vector.tensor_tensor(out=et[0:16, sbi * 128:(sbi + 1) * 128],
                                                               in0=et[0:16, sbi * 128:(sbi + 1) * 128], in1=mask8[:, :],
                                                               op=ALU.mult), [EB, K2], [EB])
            run_now(make_block(qk, av, 256, 16, zb[0:16, 0:1], m16, bC, bO, 0, []))
            for k_ in range(8):
                if k_ + 1 < 8:
                    issue_panel(k_ + 1)
                views = [cache_views(sbi, k_) for sbi in range(2)]
                for tl_ in (1, 0):
                    qk = [(sbi * 128 + h * 16, 16, views[sbi][0][:, h, tl_ * 128:(tl_ + 1) * 128],
                           qT[:, h, qcs[sbi]:qcs[sbi] + 16], views[sbi][2] + [QB[h]]) for sbi in range(2) for h in range(NH)]
                    av = [(sbi * 128 + h * 16, 16, views[sbi][1][:, tl_, h * 128:(h + 1) * 128], sbi * 128 + h * 16,
                           views[sbi][2]) for sbi in range(2) for h in range(NH)]
                    run_now(make_block(qk, av, 256, 128, zb[:, 0:1], None, bC, bO, 0, []))
            for sbi in range(2):
                for h in range(NH):
                    vec(lambda sbi=sbi, h=h: nc.vector.tensor_copy(
                        out=oT[:, h, qcs[sbi]:qcs[sbi] + 16], in_=banks[bO][:, sbi * 128 + h * 16: sbi * 128 + (h + 1) * 16]),
                        [bank_buf[bO]], [OB[h]])
            ps_release(bC); ps_release(bO)
            S.barrier()
        pq.close()
        if KSTOP <= 4:
            return

        with ExitStack() as pc:
            xn = sb("xn", [128, 16, NT], BF16, pc)
            XN = [[Buf(f"xn{c}_{g}") for g in range(3)] for c in range(16)]
            ca = sb("ca", [128, 8, NT], BF16, pc)
            CA = [Buf(f"ca{j}") for j in range(8)]
            vec(lambda: nc.vector.memset(ca[:], 0.0), [], CA)
            with ExitStack() as pc1:
                scr = norm_scratch(pc1)
                norm(1, xn, XN, scr)
                UW = NT + 4
                ub = sb("ub", [128, UW], F32, pc1)
                ccs = sb("ccs", [128, NT], F32, pc1)
                cbs = sb("cbs", [128, NT], F32, pc1)
                acc = sb("acc", [128, UW], F32, pc1)
                UB, CCS, CBS, ACC = Buf("ub"), Buf("ccs"), Buf("cbs"), Buf("acc")
                CONVO = Buf("convo")
                pieces = {0: [(0, 353, 0)], 1: [(353, 706, 353)],
                          2: [(706, 1026, 706), (1026, 1042, 1028), (1042, 1058, 1046)]}
                for jp in range(4):
                    pan = []
                    for kind in range(3):
                        pan.append(wload(wpanel(w_in, 0, D, kind * 1024 + jp * 256, 256), 16, 256))
                    for jj in range(2):
                        j = jp * 2 + jj
                        for sbi in range(2):
                            uc = 1026 + sbi * 18
                            S.add("pool", lambda j=j, sbi=sbi, uc=uc: nc.gpsimd.tensor_copy(
                                out=ub[:, uc:uc + 2], in_=state_t[:, j * 4 + sbi * 2: j * 4 + sbi * 2 + 2]), [CST], [UB])
                        for g, (a, b_) in enumerate(CG):
                            n = b_ - a
                            bb = ps_alloc(); bc = ps_alloc(); bx = ps_alloc()
                            for kind, bk in ((0, bb), (1, bc), (2, bx)):
                                wv, wb = pan[kind]
                                for kc in range(16):
                                    mm(banks[bk][:, 0:n], wv[:, kc, jj * 128:(jj + 1) * 128], xn[:, kc, a:b_],
                                       [wb, XN[kc][g]], [bank_buf[bk]], start=(kc == 0), stop=(kc == 15))
                            act(cbs[:, a:b_], banks[bb][:, 0:n], AF.Copy, [bank_buf[bb]], [CBS])
                            act(ccs[:, a:b_], banks[bc][:, 0:n], AF.Copy, [bank_buf[bc]], [CCS])
                            for (ta, tb, uc) in pieces[g]:
                                vec(lambda ta=ta, tb=tb, uc=uc, a=a, bx=bx: nc.vector.tensor_tensor(
                                    out=ub[:, uc:uc + (tb - ta)], in0=ccs[:, ta:tb], in1=banks[bx][:, ta - a:tb - a],
                                    op=ALU.mult), [CCS, bank_buf[bx]], [UB])
                        w0 = convw_t[:, j * 3 + 0:j * 3 + 1]; w1 = convw_t[:, j * 3 + 1:j * 3 + 2]; w2 = convw_t[:, j * 3 + 2:j * 3 + 3]
                        L = UW - 2
                        vec(lambda w2=w2, L=L: nc.vector.tensor_scalar(out=acc[:, 2:UW], in0=ub[:, 2:UW], scalar1=w2, scalar2=None,
                                                                       op0=ALU.mult), [UB, CST], [ACC])
                        vec(lambda w1=w1, L=L: nc.vector.scalar_tensor_tensor(out=acc[:, 2:UW], in0=ub[:, 1:UW - 1], scalar=w1,
                                                                             in1=acc[:, 2:UW], op0=ALU.mult, op1=ALU.add),
                            [UB, ACC, CST], [ACC])
                        vec(lambda w0=w0, L=L: nc.vector.scalar_tensor_tensor(out=acc[:, 2:UW], in0=ub[:, 0:UW - 2], scalar=w0,
                                                                             in1=acc[:, 2:UW], op0=ALU.mult, op1=ALU.add),
                            [UB, ACC, CST], [ACC])
                        for (ta, tb, uc) in ((2, 1026, 2), (1026, 1042, 1028), (1042, 1058, 1046)):
                            vec(lambda j=j, ta=ta, tb=tb, uc=uc: nc.vector.tensor_tensor(
                                out=ca[:, j, ta:tb], in0=cbs[:, ta:tb], in1=acc[:, uc:uc + (tb - ta)], op=ALU.mult),
                                [CBS, ACC], [CA[j]])
                        for k_, uc in enumerate((1024, 1042, 1060)):
                            vec(lambda j=j, k_=k_, uc=uc: nc.vector.tensor_copy(
                                out=convo_t[:, j * 6 + k_ * 2: j * 6 + k_ * 2 + 2], in_=ub[:, uc:uc + 2]), [UB], [CONVO])
                dma("sp", convo, convo_t[:], [CONVO], [], group="convout")
                S.barrier()

            with ExitStack() as pc2:
                mix = [sb(f"mix{i}", [128, 4, NT], BF16, pc2) for i in range(2)]
                MIX = [[[Buf(f"mix{i}_{j}_{g}") for g in range(3)] for j in range(4)] for i in range(2)]
                sga = [sb(f"sga{i}", [128, 353], F32, pc2) for i in range(3)]
                sgb = [sb(f"sgb{i}", [128, 353], F32, pc2) for i in range(3)]
                t1 = [sb(f"t1_{i}", [128, 353], F32, pc2) for i in range(3)]
                t2 = [sb(f"t2_{i}", [128, 353], F32, pc2) for i in range(1)]
                SGA = [Buf(f"sga{i}") for i in range(3)]; SGB = [Buf(f"sgb{i}") for i in range(3)]
                T1 = [Buf(f"t1_{i}") for i in range(3)]; T2 = [Buf("t2_0")]
                cnt = [0]

                def gate_group(g4):
                    s = g4 % 2
                    for j in range(4):
                        m = g4 * 4 + j
                        for kind in range(4):
                            if kind == 0:
                                wv, wb = wload(wpanel(w_in, 0, D, 6144 + m * 128, 128), 16, 128)
                            elif kind == 1:
                                wv, wb = wload(wpanel(w_in, 0, D, 8192 + m * 128, 128), 16, 128)
                            elif kind == 2:
                                wv, wb = wload(wpanel(w_co, 0, 1024, m * 128, 128), 8, 128)
                            else:
                                wv, wb = wload(wpanel(w_ao, 0, 1024, m * 128, 128), 8, 128)
                            nk = 16 if kind < 2 else 8
                            for g, (a, b_) in enumerate(CG):
                                n = b_ - a
                                bk = ps_alloc()
                                for kc in range(nk):
                                    if kind < 2:
                                        rhs, rb = xn[:, kc, a:b_], XN[kc][g]
                                    elif kind == 2:
                                        rhs, rb = ca[:, kc, a:b_], CA[kc]
                                    else:
                                        rhs, rb = oT[:, kc, a:b_], OB[kc]
                                    mm(banks[bk][:, 0:n], wv[:, kc, :], rhs, [wb, rb], [bank_buf[bk]],
                                       start=(kc == 0), stop=(kc == nk - 1))
                                if kind == 0:
                                    act(sga[g][:, 0:n], banks[bk][:, 0:n], AF.Sigmoid, [bank_buf[bk]], [SGA[g]])
                                elif kind == 1:
                                    act(sgb[g][:, 0:n], banks[bk][:, 0:n], AF.Sigmoid, [bank_buf[bk]], [SGB[g]])
                                elif kind == 2:
                                    vec(lambda g=g, n=n, bk=bk: nc.vector.tensor_tensor(
                                        out=t1[g][:, 0:n], in0=sga[g][:, 0:n], in1=banks[bk][:, 0:n], op=ALU.mult),
                                        [SGA[g], bank_buf[bk]], [T1[g]])
                                else:
                                    t = 0
                                    vec(lambda g=g, n=n, bk=bk, t=t: nc.vector.tensor_tensor(
                                        out=t2[t][:, 0:n], in0=sgb[g][:, 0:n], in1=banks[bk][:, 0:n], op=ALU.mult),
                                        [SGB[g], bank_buf[bk]], [T2[t]])
                                    S.add("pool", lambda t=t, n=n, s=s, j=j, a=a, b_=b_, g=g: nc.gpsimd.tensor_tensor(
                                        out=mix[s][:, j, a:b_], in0=t1[g][:, 0:n], in1=t2[t][:, 0:n], op=ALU.add),
                                        [T1[g], T2[t]], [MIX[s][j][g]])

                def wo_group(g4):
                    s = g4 % 2
                    for half in range(2):
                        dv, db_ = wload(wpanel(w_o, g4 * 512, 512, half * 1024, 1024), 4, 1024)
                        for mo in range(8):
                            m = half * 8 + mo
                            for g, (a, b_) in enumerate(CG):
                                n = b_ - a
                                bk = ps_alloc()
                                for kc in range(4):
                                    mm(banks[bk][:, 0:n], dv[:, kc, mo * 128:(mo + 1) * 128], mix[s][:, kc, a:b_],
                                       [db_, MIX[s][kc][g]], [bank_buf[bk]], start=(kc == 0), stop=(kc == 3))
                                vec(lambda m=m, a=a, b_=b_, n=n, bk=bk: nc.vector.tensor_tensor(
                                    out=xres[:, m, a:b_], in0=banks[bk][:, 0:n], in1=xres[:, m, a:b_], op=ALU.add),
                                    [bank_buf[bk], XB[m][g]], [XB[m][g]])

                for g4 in range(4):
                    gate_group(g4)
                    if g4 >= 1:
                        wo_group(g4 - 1)
                wo_group(3)
                S.barrier()

    if KSTOP <= 5:
        return
    with ExitStack() as ph3:
        xn = sb("xn", [128, 16, NT], BF16, ph3)
        XN = [[Buf(f"xn{c}_{g}") for g in range(3)] for c in range(16)]
        scr = norm_scratch(ph3)
        norm(2, xn, XN, scr)
        ffn(xn, XN, w_gu2, w_dn2, ph3)
        yst = [sb(f"yst{i}", [128, 353], F32, ph3) for i in range(4)]
        YST = [Buf(f"yst{i}") for i in range(4)]
        norm(3, None, None, scr, final_out=(yst, YST), per_cg=True)


_CACHE = {}


def _host_inputs(x_prompt, x_sample, cache_k, cache_v, state_conv, norm_ffn1, ffn1_w_gate_up, ffn1_w_down,
                 norm_mix, w_in, conv_w, w_conv_out, w_attn_out, w_o, norm_ffn2, ffn2_w_gate_up, ffn2_w_down,
                 norm_final):
    f = lambda a: np.ascontiguousarray(np.asarray(a, dtype=np.float32))
    x_prompt, x_sample = f(x_prompt), f(x_sample)
    cache_k, cache_v, state_conv = f(cache_k)[0], f(cache_v)[0], f(state_conv)[0]
    gl = np.stack([f(norm_ffn1)[0], f(norm_mix)[0], f(norm_ffn2)[0], f(norm_final)], 0)
    gains = np.ascontiguousarray(gl.reshape(4, 16, 128).transpose(2, 0, 1).reshape(128, 64))
    cw = f(conv_w)[0]
    convw = np.ascontiguousarray(cw.reshape(3, 8, 128).transpose(2, 1, 0).reshape(128, 24))
    shared = {
        "w_gu1": f(ffn1_w_gate_up)[0], "w_dn1": f(ffn1_w_down)[0], "w_in": f(w_in)[0],
        "w_co": f(w_conv_out)[0], "w_ao": f(w_attn_out)[0], "w_o": f(w_o)[0],
        "w_gu2": f(ffn2_w_gate_up)[0], "w_dn2": f(ffn2_w_down)[0], "gains": gains, "convw": convw,
    }
    maps = []
    for c in range(8):
        b, half = c // 2, c % 2
        xt = np.zeros((NT, D), np.float32)
        if half == 1:
            xt[0:2] = x_prompt[b, 1022:1024]
        xt[2:1026] = x_prompt[b, half * 1024:(half + 1) * 1024]
        xt[1026:1042] = x_sample[2 * c]
        xt[1042:1058] = x_sample[2 * c + 1]
        st = state_conv[2 * c:2 * c + 2]
        stT = st.reshape(2, 2, 8, 128).transpose(3, 2, 0, 1).reshape(128, 32)
        m = dict(shared)
        m["xT"] = np.ascontiguousarray(xt.T)
        m["stateT"] = np.ascontiguousarray(stT)
        m["flags"] = np.full((128, 1), 0.0 if half == 1 else NEG, np.float32)
        ck = cache_k[2 * c:2 * c + 2].reshape(2, 8, 256, NH, 128)
        m["ckT"] = np.ascontiguousarray(ck.transpose(0, 1, 4, 3, 2)).reshape(2, 8, 128, NH * 256)
        m["cv"] = np.ascontiguousarray(cache_v[2 * c:2 * c + 2].reshape(2, 2048, 1024))
        maps.append(m)
    return maps


def kernel(**inputs):
    if "nc" not in _CACHE:
        _CACHE["nc"] = build_program()
    nc = _CACHE["nc"]
    maps = _host_inputs(**inputs)
    res = run_bass_kernel_spmd(nc, maps, core_ids=list(range(8)))
    R = res.results
    y_prompt = np.zeros((4, 2048, D), np.float32)
    y_sample = np.zeros((16, 16, D), np.float32)
    k_prompt = np.zeros((1, 4, 2048, NH, 128), np.float32)
    v_prompt = np.zeros((1, 4, 2048, NH, 128), np.float32)
    conv_prompt = np.zeros((1, 4, 2, 1024), np.float32)
    k_sample = np.zeros((1, 16, 16, NH, 128), np.float32)
    v_sample = np.zeros((1, 16, 16, NH, 128), np.float32)
    conv_sample = np.zeros((1, 16, 2, 1024), np.float32)
    for c in range(8):
        b, half = c // 2, c % 2
        yT = R[c]["yT"]; kT = R[c]["kTo"]; vT = R[c]["vTo"]
        cvo = R[c]["convo"].reshape(128, 8, 3, 2)
        sl = slice(half * 1024, (half + 1) * 1024)
        y_prompt[b, sl] = yT[:, 2:1026].T
        k_prompt[0, b, sl] = kT[:, 2:1026].T.reshape(1024, NH, 128)
        v_prompt[0, b, sl] = vT[:, 2:1026].T.reshape(1024, NH, 128)
        cflat = cvo.transpose(2, 3, 1, 0).reshape(3, 2, 1024)
        if half == 1:
            conv_prompt[0, b] = cflat[0]
        for s in range(2):
            sbi = 2 * c + s
            cols = slice(1026 + 16 * s, 1042 + 16 * s)
            y_sample[sbi] = yT[:, cols].T
            k_sample[0, sbi] = kT[:, cols].T.reshape(16, NH, 128)
            v_sample[0, sbi] = vT[:, cols].T.reshape(16, NH, 128)
            conv_sample[0, sbi] = cflat[1 + s]
    return (y_prompt, y_sample, k_prompt, v_prompt, conv_prompt, k_sample, v_sample, conv_sample)
```

```python
import os
import numpy as np
from contextlib import ExitStack
import concourse.bass as bass
import concourse.mybir as mybir
from concourse.bass_utils import run_bass_kernel_spmd

F32 = mybir.dt.float32
BF16 = mybir.dt.bfloat16
AF = mybir.ActivationFunctionType
ALU = mybir.AluOpType

D = 2048
DFF = 5632
DIN = 10240
NH = 8
NT = 1058
CG = [(0, 353), (353, 706), (706, 1058)]
PCOL = 2
SCOL = 1026
EPS = 1e-6
SCALE = 128 ** -0.5
NSLOT = 5
NEG = -30000.0
KSTOP = int(os.environ.get("KSTOP", "99"))
KCORES = int(os.environ.get("KCORES", "8"))


DBG = {}


class _Stop(Exception):
    pass


def _stop(n):
    if KSTOP <= n:
        raise _Stop()


class Buf:
    __slots__ = ("name", "w", "r")

    def __init__(self, name):
        self.name = name
        self.w = None
        self.r = []


class Op:
    __slots__ = ("eng", "fn", "deps", "dma", "group", "gidx", "marked", "rank", "ndep", "cc", "bar", "epoch")

    def __init__(self, eng, fn, dma=False, group=None, cc=False):
        self.eng = eng
        self.fn = fn
        self.deps = []
        self.dma = dma
        self.group = group
        self.gidx = 0
        self.marked = False
        self.rank = 0
        self.ndep = 0
        self.cc = cc
        self.bar = False
        self.epoch = 0


class Sched:
    STRICT = ("act", "dve", "pool")

    def __init__(self, nc, es):
        self.nc = nc
        self.es = es
        self.ops = []
        self.h = {"pe": nc.tensor, "act": nc.scalar, "dve": nc.vector, "pool": nc.gpsimd, "sp": nc.sync}
        self.groups = {}
        self.last = {}
        self.since_barrier = []

    def add(self, eng, fn, reads=(), writes=(), dma=False, group=None, mode="seq", pos=None, cc=False):
        op = Op(eng, fn, dma=dma, group=group, cc=cc)
        deps = []
        for b in reads:
            if b.w is not None:
                deps.append((b.w, True))
        for b in writes:
            if b.w is not None:
                if not (dma and b.w.dma and b not in reads):
                    deps.append((b.w, False))
            for r in b.r:
                deps.append((r, False))
        seen = set()
        for d, raw in deps:
            if d is op or id(d) in seen:
                continue
            if not d.dma and d.eng == eng and not dma:
                if eng not in self.STRICT:
                    continue
            if not d.dma and d.eng == eng and dma:
                continue
            seen.add(id(d))
            op.deps.append(d)
            d.ndep += 1
        if dma:
            for b in list(reads) + list(writes):
                cands = ([b.w] if b.w is not None else []) + (list(b.r) if b in writes else [])
                for d in cands:
                    if d is not None and not d.dma and d.eng == eng and id(d) not in seen:
                        seen.add(id(d))
                        op.deps.append(d)
                        d.ndep += 1
        for b in reads:
            b.r.append(op)
        for b in writes:
            b.w = op
            b.r = []
        if dma or cc:
            g = self.groups.setdefault(group, [mode, 0])
            g[1] += 1
            op.gidx = g[1]
        if pos is None:
            self.ops.append(op)
        else:
            self.ops.insert(pos, op)
        if not dma and not cc:
            self.last[eng] = op
        self.since_barrier.append(op)
        return op

    def index_after(self, op):
        return self.ops.index(op) + 1

    def barrier(self, skip_pool=False):
        lasts = dict(self.last)
        if skip_pool:
            lasts.pop("pool", None)
        lastg = {}
        for o in self.since_barrier:
            if o.dma:
                lastg[o.group] = o
        pend = list(lastg.values())
        for e in ("pe", "act", "dve", "pool", "sp"):
            op = Op(e, None)
            op.bar = True
            for e2, l in lasts.items():
                if e2 != e:
                    op.deps.append(l)
                    l.ndep += 1
            for p in pend:
                op.deps.append(p)
                p.ndep += 1
            self.ops.append(op)
        self.since_barrier = []

    def emit(self):
        nc, es = self.nc, self.es
        ep = 0
        prev_bar = False
        for op in self.ops:
            if prev_bar and not op.bar:
                ep += 1
            op.epoch = ep
            prev_bar = op.bar
        nep = ep + 1
        esem = {(e, k): es.enter_context(nc.semaphore(f"e_{e}_{k}")) for e in self.h for k in range(nep)}
        gsem = {g: es.enter_context(nc.semaphore("g_" + g)) for g in self.groups}
        for op in self.ops:
            for d in op.deps:
                if not d.dma and not d.cc and (d.epoch == op.epoch or d.eng == "pool"):
                    d.marked = True
        cnt = {(e, k): 0 for e in self.h for k in range(nep)}
        for op in self.ops:
            if op.marked:
                cnt[(op.eng, op.epoch)] += 1
                op.rank = cnt[(op.eng, op.epoch)]
        seen = {e: {} for e in self.h}
        for op in self.ops:
            need = {}
            for d in op.deps:
                if d.cc:
                    key, val = ("g", d.group), d.gidx
                elif d.dma:
                    mode, total = self.groups[d.group]
                    key, val = ("g", d.group), 16 * (total if mode == "all" else d.gidx)
                else:
                    if d.epoch != op.epoch and d.eng != "pool":
                        assert d.epoch < op.epoch
                        continue
                    key, val = ("e", (d.eng, d.epoch)), d.rank
                if need.get(key, 0) < val:
                    need[key] = val
            hnd = self.h[op.eng]
            for key, val in need.items():
                if seen[op.eng].get(key, 0) >= val:
                    continue
                seen[op.eng][key] = val
                sem = gsem[key[1]] if key[0] == "g" else esem[key[1]]
                hnd.wait_ge(sem, val)
            if op.fn is None:
                continue
            ins = op.fn()
            if op.cc:
                ins.then_inc(gsem[op.group])
            elif op.dma:
                ins.then_inc(gsem[op.group], 16)
            elif op.marked:
                ins.then_inc(esem[(op.eng, op.epoch)], 1)
        sp = self.h["sp"]
        self.final_counts = dict(cnt)
        for g, (mode, total) in self.groups.items():
            if not g.startswith("cc"):
                sp.wait_ge(gsem[g], 16 * total)
        for (e, k), c in cnt.items():
            if c and k == nep - 1:
                sp.wait_ge(esem[(e, k)], c)


def build_program():
    nc = bass.Bass("TRN2", target_bir_lowering=False)
    es = ExitStack()
    S = Sched(nc, es)

    def din(name, shape, dt=F32):
        return nc.dram_tensor(name, list(shape), dt, kind="ExternalInput").ap()

    def dout(name, shape, dt=F32):
        return nc.dram_tensor(name, list(shape), dt, kind="ExternalOutput").ap()

    xT = din("xT", [D, NT])
    w_gu1 = din("w_gu1", [D, 2 * DFF]); w_dn1 = din("w_dn1", [DFF, D])
    w_in = din("w_in", [D, DIN])
    w_co = din("w_co", [1024, D]); w_ao = din("w_ao", [1024, D]); w_o = din("w_o", [D, D])
    w_gu2 = din("w_gu2", [D, 2 * DFF]); w_dn2 = din("w_dn2", [DFF, D])
    gains = din("gains", [128, 4 * 16])
    convw = din("convw", [128, 8 * 3])
    stateT = din("stateT", [128, 8 * 4])
    flags = din("flags", [128, 1])
    ckT = din("ckT", [2, 8, 128, NH * 256])
    cv = din("cv", [2, 2048, 1024])
    yT = dout("yT", [D, NT]); kTo = dout("kTo", [1024, NT]); vTo = dout("vTo", [1024, NT])
    convo = dout("convo", [128, 8 * 6])
    srcs = [[nc.dram_tensor(f"kv_src{kv}_{hp}", [256, 1024], BF16) for hp in range(4)] for kv in range(2)]
    gats = [[nc.dram_tensor(f"kv_gat{kv}_{hp}", [512, 1024], BF16) for hp in range(4)] for kv in range(2)]

    uniq = [0]

    def sb(name, shape, dt, ctx=None):
        uniq[0] += 1
        DBG[name] = f"{name}_{uniq[0]}"
        return (ctx or es).enter_context(nc.sbuf_tensor(f"{name}_{uniq[0]}", list(shape), dt))

    xres = sb("xres", [128, 16, NT], F32)
    ring = [sb(f"ring{i}", [128, 4096], BF16) for i in range(NSLOT)]
    gain_t = sb("gain_t", [128, 64], F32)
    convw_t = sb("convw_t", [128, 24], F32)
    state_t = sb("state_t", [128, 32], F32)
    flag_t = sb("flag_t", [128, 1], F32)
    iota_t = sb("iota_t", [128, 128], F32)
    tri = sb("tri", [128, 128], BF16)
    utri = sb("utri", [128, 128], BF16)
    ident = sb("ident", [128, 128], BF16)
    maskf = sb("maskf", [128, 128], F32)
    mask8 = sb("mask8", [16, 128], F32)
    ones_b = sb("ones_b", [128, 128], BF16)
    convo_t = sb("convo_t", [128, 48], F32)
    relay_t = sb("relay_t", [128, 8], F32)

    banks = [es.enter_context(nc.psum_tensor(f"bank{i}", [128, 512], F32)) for i in range(8)]
    bank_buf = [Buf(f"bank{i}") for i in range(8)]
    held = set()
    ps_state = {"i": 0}

    def ps_alloc(hold=False):
        for _ in range(16):
            i = ps_state["i"] % 8
            ps_state["i"] += 1
            if i not in held:
                if hold:
                    held.add(i)
                return i
        raise RuntimeError("no psum bank")

    def ps_release(i):
        held.discard(i)

    def mm(out, lhsT, rhs, reads, writes, start=True, stop=True, skip=False):
        if skip:
            return S.add("pe", lambda: nc.tensor.matmul(out, lhsT, rhs, start=start, stop=stop, skip_group_check=True),
                         reads, writes)
        return S.add("pe", lambda: nc.tensor.matmul(out, lhsT, rhs, start=start, stop=stop), reads, writes)

    def act(out, in_, func, reads, writes, **kw):
        return S.add("act", lambda: nc.scalar.activation(out=out, in_=in_, func=func, **kw), reads, writes)

    def vec(fn, reads, writes, eng="dve"):
        return S.add(eng, fn, reads, writes)

    def dma(eng, out, in_, reads, writes, group, mode="seq", pos=None):
        h = S.h[eng]
        return S.add(eng, lambda: h.dma_start(out=out, in_=in_), reads, writes, dma=True, group=group, mode=mode, pos=pos)

    slot_buf = [Buf(f"slot{i}") for i in range(NSLOT)]
    wst = {"n": 0}

    def wload(dram_view, kc, cols):
        i = wst["n"]
        wst["n"] += 1
        s = i % NSLOT
        b = slot_buf[s]
        view = ring[s][:, 0:kc * cols].rearrange("p (k c) -> p k c", k=kc)
        pos = None
        if b.r:
            pos = S.index_after(b.r[-1])
        elif b.w is not None:
            pos = S.index_after(b.w)
        elif wst.get("early") is not None:
            pos = wst["early"]
            wst["early"] += 1
        dma("pool", view, dram_view, [], [b], group=f"slot{s}", pos=pos)
        return view, b

    def wpanel(w, r0, nr, c0, ncol):
        return w[r0:r0 + nr, c0:c0 + ncol].rearrange("(k p) c -> p k c", p=128)

    CST = Buf("const")
    for t, d_ in ((gain_t, gains), (convw_t, convw), (state_t, stateT), (flag_t, flags)):
        dma("sp", t[:], d_, [], [CST], group="const", mode="all")
    XB = [[Buf(f"x{c}_{g}") for g in range(3)] for c in range(16)]
    for g, (a, b_) in enumerate(CG):
        dma("sp", xres[:, :, a:b_], xT[:, a:b_].rearrange("(c p) n -> p c n", p=128), [],
            [XB[c][g] for c in range(16)], group=f"xload{g}")
    wst["early"] = len(S.ops)
    IOT = Buf("iota")
    S.add("pool", lambda: nc.gpsimd.iota(iota_t[:], pattern=[[1, 128]], base=0, channel_multiplier=-1,
                                         allow_small_or_imprecise_dtypes=True), [], [IOT])
    K = Buf("konst")
    vec(lambda: nc.vector.tensor_single_scalar(out=tri[:], in_=iota_t[:], scalar=0.0, op=ALU.is_le), [IOT], [K])
    vec(lambda: nc.vector.tensor_single_scalar(out=utri[:], in_=iota_t[:], scalar=0.0, op=ALU.is_gt), [IOT], [K])
    vec(lambda: nc.vector.tensor_single_scalar(out=ident[:], in_=iota_t[:], scalar=0.0, op=ALU.is_equal), [IOT], [K])
    vec(lambda: nc.vector.tensor_single_scalar(out=maskf[:], in_=iota_t[:], scalar=0.0, op=ALU.is_gt), [IOT], [K])
    K2 = Buf("konst2")
    for h in range(NH):
        vec(lambda h=h: nc.vector.tensor_single_scalar(out=mask8[:, h * 16:(h + 1) * 16], in_=iota_t[0:16, 0:16],
                                                       scalar=0.0, op=ALU.is_gt), [IOT], [K2])
    vec(lambda: nc.vector.memset(ones_b[:], 1.0), [], [K])

    def norm(gidx, xn, XN, scr, final_out=None, per_cg=False):
        sq, rs, rstd = scr["sq"], scr["rs"], scr["rstd"]
        SQ, RS, RSTD = scr["SQ"], scr["RS"], scr["RSTD"]
        bks = [ps_alloc(hold=True) for _ in range(3)]
        kq = [0]

        def squares(g):
            a, b_ = CG[g]; n = b_ - a
            for c in range(16):
                s = kq[0] % len(sq); kq[0] += 1
                act(sq[s][:, 0:n], xres[:, c, a:b_], AF.Square, [XB[c][g]], [SQ[s]])
                mm(banks[bks[g]][:, 0:n], ones_b[:], sq[s][:, 0:n], [SQ[s], K], [bank_buf[bks[g]]], start=(c == 0), stop=(c == 15))

        def stats(g):
            a, b_ = CG[g]; n = b_ - a
            act(rs[g][:, 0:n], banks[bks[g]][:, 0:n], AF.Sqrt, [bank_buf[bks[g]], scr["EPSB"]], [RS[g]], scale=1.0 / D,
                bias=scr["eps"][:, 0:1])
            vec(lambda n=n, g=g: nc.vector.reciprocal(out=rstd[g][:, 0:n], in_=rs[g][:, 0:n]), [RS[g]], [RSTD[g]])

        def scaling(g):
            a, b_ = CG[g]; n = b_ - a
            for c in range(16):
                gcol = gain_t[:, gidx * 16 + c: gidx * 16 + c + 1]
                st, ST = final_out if final_out is not None else (None, None)
                if final_out is not None:
                    s_ = (g * 16 + c) % len(st)
                    o_ap, OBUF = st[s_][:, 0:n], ST[s_]
                else:
                    o_ap, OBUF = xn[:, c, a:b_], XN[c][g]
                if c % 3 != 2:
                    vec(lambda c=c, a=a, b_=b_, n=n, g=g, gcol=gcol, o_ap=o_ap: nc.vector.scalar_tensor_tensor(
                        out=o_ap, in0=xres[:, c, a:b_], scalar=gcol, in1=rstd[g][:, 0:n],
                        op0=ALU.mult, op1=ALU.mult), [XB[c][g], RSTD[g], CST], [OBUF])
                else:
                    t_ = scr["nk"][0] % 2; scr["nk"][0] += 1
                    act(scr["nt"][t_][:, 0:n], xres[:, c, a:b_], AF.Copy, [XB[c][g], CST], [scr["NT"][t_]], scale=gcol)
                    S.add("pool", lambda t_=t_, n=n, g=g, o_ap=o_ap: nc.gpsimd.tensor_tensor(
                        out=o_ap, in0=scr["nt"][t_][:, 0:n], in1=rstd[g][:, 0:n], op=ALU.mult),
                        [scr["NT"][t_], RSTD[g]], [OBUF])
                if final_out is not None:
                    dma("sp", yT[c * 128:(c + 1) * 128, a:b_], o_ap, [OBUF], [], group=f"yout{s_}")

        if per_cg:
            squares(0)
            for g in range(3):
                stats(g)
                if g + 1 < 3:
                    squares(g + 1)
                scaling(g)
        else:
            for g in range(3):
                squares(g)
            for g in range(3):
                stats(g)
            for g in range(3):
                scaling(g)
        for b in bks:
            ps_release(b)

    def norm_scratch(ctx):
        scr = {
            "sq": [sb(f"sq{i}", [128, 353], BF16, ctx) for i in range(4)],
            "rs": [sb(f"rs{i}", [128, 353], F32, ctx) for i in range(3)],
            "rstd": [sb(f"rstd{i}", [128, 353], F32, ctx) for i in range(3)],
            "eps": sb("epsc", [128, 1], F32, ctx),
            "nt": [sb(f"nt{i}", [128, 353], F32, ctx) for i in range(2)], "NT": [Buf("nt0"), Buf("nt1")], "nk": [0],
            "SQ": [Buf(f"sq{i}") for i in range(4)], "RS": [Buf(f"rs{i}") for i in range(3)], "RSTD": [Buf(f"rstd{i}") for i in range(3)],
        }
        scr["EPSB"] = Buf("epsb")
        vec(lambda: nc.vector.memset(scr["eps"][:], EPS), [], [scr["EPSB"]])
        return scr

    def ffn(xn, XN, w_gu, w_dn, ctx):
        hb = [sb(f"hb{i}", [128, 4, NT], BF16, ctx) for i in range(2)]
        HB = [[[Buf(f"hb{i}_{j}_{g}") for g in range(3)] for j in range(4)] for i in range(2)]
        tmp = [sb(f"stmp{i}", [128, 353], F32, ctx) for i in range(2)]
        TMP = [Buf("stmp0"), Buf("stmp1")]
        tcount = [0]
        NG = DFF // 512

        def phaseA(gi):
            s = gi % 2
            for half in range(2):
                c0 = gi * 512 + half * 256
                gv, gb_ = wload(wpanel(w_gu, 0, D, c0, 256), 16, 256)
                uv, ub_ = wload(wpanel(w_gu, 0, D, DFF + c0, 256), 16, 256)
                for jj in range(2):
                    j = half * 2 + jj
                    for g, (a, b_) in enumerate(CG):
                        n = b_ - a
                        bg = ps_alloc(); bu = ps_alloc()
                        for kc in range(16):
                            mm(banks[bg][:, 0:n], gv[:, kc, jj * 128:(jj + 1) * 128], xn[:, kc, a:b_],
                               [gb_, XN[kc][g]], [bank_buf[bg]], start=(kc == 0), stop=(kc == 15))
                        for kc in range(16):
                            mm(banks[bu][:, 0:n], uv[:, kc, jj * 128:(jj + 1) * 128], xn[:, kc, a:b_],
                               [ub_, XN[kc][g]], [bank_buf[bu]], start=(kc == 0), stop=(kc == 15))
                        t = tcount[0] % 2; tcount[0] += 1
                        act(tmp[t][:, 0:n], banks[bg][:, 0:n], AF.Silu, [bank_buf[bg]], [TMP[t]])
                        vec(lambda s=s, j=j, a=a, b_=b_, n=n, t=t, bu=bu: nc.vector.tensor_tensor(
                            out=hb[s][:, j, a:b_], in0=tmp[t][:, 0:n], in1=banks[bu][:, 0:n], op=ALU.mult),
                            [TMP[t], bank_buf[bu]], [HB[s][j][g]])

        def phaseB(gi):
            s = gi % 2
            for half in range(2):
                dv, db_ = wload(wpanel(w_dn, gi * 512, 512, half * 1024, 1024), 4, 1024)
                for mo in range(8):
                    m = half * 8 + mo
                    for g, (a, b_) in enumerate(CG):
                        n = b_ - a
                        bk = ps_alloc()
                        for kc in range(4):
                            mm(banks[bk][:, 0:n], dv[:, kc, mo * 128:(mo + 1) * 128], hb[s][:, kc, a:b_],
                               [db_, HB[s][kc][g]], [bank_buf[bk]], start=(kc == 0), stop=(kc == 3))
                        vec(lambda m=m, a=a, b_=b_, n=n, bk=bk: nc.vector.scalar_tensor_tensor(
                            out=xres[:, m, a:b_], in0=banks[bk][:, 0:n], scalar=0.5, in1=xres[:, m, a:b_],
                            op0=ALU.mult, op1=ALU.add), [bank_buf[bk], XB[m][g]], [XB[m][g]])

        for gi in range(NG):
            phaseA(gi)
            if gi >= 1:
                phaseB(gi - 1)
        phaseB(NG - 1)

    try:
        _build_phases(locals())
    except _Stop:
        pass
    S.emit()
    if os.environ.get("KDEBUG"):
        print("marked counts", S.final_counts, "groups", {g: v[1] for g, v in S.groups.items()})
    es.close()
    return nc


def _build_phases(L):
    globals_ = globals()
    for k_, v_ in L.items():
        globals_[k_] = v_
    with ExitStack() as ph:
        xn = sb("xn", [128, 16, NT], BF16, ph)
        XN = [[Buf(f"xn{c}_{g}") for g in range(3)] for c in range(16)]
        scr = norm_scratch(ph)
        norm(0, xn, XN, scr, per_cg=True)
        if KSTOP <= 0:
            return
        ffn(xn, XN, w_gu1, w_dn1, ph)
        S.barrier()
    if KSTOP <= 1:
        return

    with ExitStack() as ph2:
        oT = sb("oT", [128, NH, NT], BF16, ph2)
        ksamp = sb("ksamp", [128, NH, 32], BF16, ph2)
        vsamp = sb("vsamp", [128, NH, 32], BF16, ph2)
        pq = ph2.enter_context(ExitStack())
        qT = sb("qT", [128, NH, NT], BF16, pq)
        QB = [Buf(f"q{h}") for h in range(NH)]
        OB = [Buf(f"o{h}") for h in range(NH)]
        KS = Buf("ksamp"); VS = Buf("vsamp")
        vec(lambda: nc.vector.memset(oT[:], 0.0), [], OB)

        SRCS = [Buf("kvsrc0"), Buf("kvsrc1")]
        GATS = [[Buf(f"gat{kv}_{hp}") for hp in range(4)] for kv in range(2)]
        RELAY = [[Buf(f"relay{kv}_{hp}") for hp in range(4)] for kv in range(2)]
        SRCB = [[Buf(f"srcb{kv}_{h}") for h in range(NH)] for kv in range(2)]
        with ExitStack() as pa:
            xn = sb("xn", [128, 16, NT], BF16, pa)
            XN = [[Buf(f"xn{c}_{g}") for g in range(3)] for c in range(16)]
            scr = norm_scratch(pa)
            norm(1, xn, XN, scr)
            st32 = [sb(f"st32_{i}", [128, NT], F32, pa) for i in range(2)]
            st16 = [sb(f"st16_{i}", [128, NT], BF16, pa) for i in range(2)]
            ST32 = [Buf("st32_0"), Buf("st32_1")]; ST16 = [Buf("st16_0"), Buf("st16_1")]
            sc = [0]
            for kind in range(int(os.environ.get("KKINDS", "3"))):
                for pn in range(4):
                    c0 = 3072 + kind * 1024 + pn * 256
                    wv, wb = wload(wpanel(w_in, 0, D, c0, 256), 16, 256)
                    for jj in range(2):
                        h = pn * 2 + jj
                        s = sc[0] % 2; sc[0] += 1
                        for g, (a, b_) in enumerate(CG):
                            n = b_ - a
                            bk = ps_alloc()
                            for kc in range(16):
                                mm(banks[bk][:, 0:n], wv[:, kc, jj * 128:(jj + 1) * 128], xn[:, kc, a:b_],
                                   [wb, XN[kc][g]], [bank_buf[bk]], start=(kc == 0), stop=(kc == 15))
                            if kind == 0:
                                act(qT[:, h, a:b_], banks[bk][:, 0:n], AF.Copy, [bank_buf[bk]], [QB[h]])
                            else:
                                act(st32[s][:, a:b_], banks[bk][:, 0:n], AF.Copy, [bank_buf[bk]], [ST32[s]])
                                vec(lambda s=s, a=a, b_=b_: nc.vector.tensor_copy(
                                    out=st16[s][:, a:b_], in_=st32[s][:, a:b_]), [ST32[s]], [ST16[s]])
                        if kind > 0:
                            od = kTo if kind == 1 else vTo
                            if not os.environ.get("KNOOUT"):
                                dma("sp", od[h * 128:(h + 1) * 128, :], st32[s][:], [ST32[s]], [], group=f"kvout{s}")
                            r0 = (h % 2) * 128
                            sd = srcs[kind - 1][h // 2]
                            if not os.environ.get("KNOSRC"):
                                dma("sp", sd[r0:r0 + 128, :], st16[s][:, PCOL:PCOL + 1024], [ST16[s]], [SRCB[kind - 1][h]], group=f"kvsrc{s}")
                            dst, DB = (ksamp, KS) if kind == 1 else (vsamp, VS)
                            if not os.environ.get("KNOKS"):
                                vec(lambda s=s, h=h, dst=dst: nc.vector.tensor_copy(out=dst[:, h, :], in_=st16[s][:, SCOL:NT]),
                                    [ST16[s]], [DB])
                            if h % 2 == 1 and not os.environ.get("KNOCC"):
                                kv, hp_ = kind - 1, h // 2
                                S.add("pool", lambda kv=kv, hp_=hp_: nc.gpsimd.collective_compute(
                                    "AllGather", ALU.bypass, replica_groups=[[2 * i, 2 * i + 1] for i in range(KCORES // 2)],
                                    ins=[srcs[kv][hp_].ap().opt()], outs=[gats[kv][hp_].ap().opt()]),
                                    [SRCB[kv][h - 1], SRCB[kv][h]], [GATS[kv][hp_]], group="cc", cc=True)
                                S.add("pool", lambda: nc.gpsimd.memset(relay_t[:], 0.0), [GATS[kv][hp_]], [RELAY[kv][hp_]])
            S.barrier(skip_pool=True)
        if KSTOP <= 2:
            return

        with ExitStack() as pb:
            KB = [sb(f"KB{i}", [128, 4096], BF16, pb) for i in range(2)]
            VB = [sb(f"VB{i}", [128, 4096], BF16, pb) for i in range(2)]
            VT = sb("VTt", [128, 4096], BF16, pb)
            NB = 3
            e_t = [sb(f"e{i}", [128, 512], F32, pb) for i in range(NB)]
            sp_t = [sb(f"sp{i}", [128, 512], BF16, pb) for i in range(NB)]
            w_t = [sb(f"w{i}", [128, 512], F32, pb) for i in range(NB)]
            a_t = [sb(f"a{i}", [128, 512], BF16, pb) for i in range(NB)]
            Eb = [Buf(f"e{i}") for i in range(NB)]; SPb = [Buf(f"sp{i}") for i in range(NB)]
            Wb = [Buf(f"w{i}") for i in range(NB)]; Ab = [Buf(f"a{i}") for i in range(NB)]
            vstok = sb("vstok", [16, NH, 128], BF16, pb)
            vstok2 = sb("vstok2", [16, NH, 128], BF16, pb)
            VSTOK = [Buf("vstok"), Buf("vstok2")]
            zb = sb("zerob", [128, 1], F32, pb)
            ZB = Buf("zb")
            vec(lambda: nc.vector.memset(zb[:], 0.0), [], [ZB])
            blk = [0]

            def run_pipeline(tasks):
                nst = 6
                n = len(tasks)
                for t_ in tasks:
                    t_["ctx"]["i"] = blk[0] % NB
                    blk[0] += 1
                for step in range(n + nst - 1):
                    for s in (5, 2, 4, 3, 1, 0):
                        t = step - s
                        if 0 <= t < n:
                            tasks[t]["st"][s]()

            def make_block(qk_list, av_list, ncol, kp, bias_ap, mask_fn, bC, bO, c0, extra_reads):
                st = {"i": None}

                def s0():
                    st["bz"] = ps_alloc()
                    bz = st["bz"]
                    for (oc, n_, l, r, rd) in qk_list:
                        mm(banks[bz][0:kp, oc:oc + n_], l, r, rd, [bank_buf[bz]])

                def s1():
                    bz = st["bz"]
                    act(e_t[st["i"]][0:kp, 0:ncol], banks[bz][0:kp, 0:ncol], AF.Exp, [bank_buf[bz], CST, ZB], [Eb[st["i"]]],
                        scale=SCALE, bias=bias_ap)
                    if mask_fn is not None:
                        mask_fn(e_t[st["i"]], Eb[st["i"]])

                def s2():
                    act(sp_t[st["i"]][0:kp, 0:ncol], e_t[st["i"]][0:kp, 0:ncol], AF.Ln, [Eb[st["i"]]], [SPb[st["i"]]], bias=1.0)

                def s3():
                    mm(banks[bC][0:kp, c0:c0 + ncol], tri[0:kp, 0:kp], sp_t[st["i"]][0:kp, 0:ncol], [SPb[st["i"]], K],
                       [bank_buf[bC]], start=False, stop=True, skip=True)

                def s4():
                    act(w_t[st["i"]][0:kp, 0:ncol], banks[bC][0:kp, c0:c0 + ncol], AF.Exp, [bank_buf[bC]], [Wb[st["i"]]], scale=-1.0)
                    vec(lambda: nc.vector.tensor_tensor(out=a_t[st["i"]][0:kp, 0:ncol], in0=e_t[st["i"]][0:kp, 0:ncol],
                                                        in1=w_t[st["i"]][0:kp, 0:ncol], op=ALU.mult), [Eb[st["i"]], Wb[st["i"]]], [Ab[st["i"]]])

                def s5():
                    mm(banks[bC][:, c0:c0 + ncol], utri[0:kp, :], sp_t[st["i"]][0:kp, 0:ncol], [SPb[st["i"]], K],
                       [bank_buf[bC]], start=False, stop=True, skip=True)
                    for (oc, n_, l, ac, rd) in av_list:
                        mm(banks[bO][:, oc:oc + n_], l, a_t[st["i"]][0:kp, ac:ac + n_], [Ab[st["i"]]] + rd + extra_reads,
                           [bank_buf[bO]], start=False, stop=True, skip=True)

                return {"st": [s0, s1, s2, s3, s4, s5], "ctx": st}

            def diag_mask(sub0):
                def f(et, EB):
                    vec(lambda: nc.vector.tensor_tensor(out=et[:, sub0:sub0 + 128], in0=et[:, sub0:sub0 + 128],
                                                        in1=maskf[:], op=ALU.mult), [EB, K], [EB])
                return f

            KBO = [Buf("KBO0"), Buf("KBO1")]; KBT = [Buf("KBT0"), Buf("KBT1")]
            VBO = [Buf("VBO0"), Buf("VBO1")]; VBT = [Buf("VBT0"), Buf("VBT1")]
            VTO = Buf("VTO"); VTT = Buf("VTT")
            KBb = [[KBO[0], KBT[0]], [KBO[1], KBT[1]]]
            VBb = [[VBO[0], VBT[0]], [VBO[1], VBT[1]]]

            def pair_views(hp):
                kb_, vb_ = KB[hp % 2], VB[hp % 2]
                kview = kb_[:].rearrange("p (h k) -> p h k", h=2)
                vview = vb_[:].rearrange("p (h t d) -> p h t d", h=2, t=16)
                return kview, vview

            vtv = VT[:].rearrange("p (h k) -> p h k", h=2)

            def load_pair(hp):
                kview, vview = pair_views(hp)
                sl = hp % 2
                for hh in range(2):
                    rr = slice(hh * 128, (hh + 1) * 128)
                    dma("sp", kview[:, hh, 1024:2048], srcs[0][hp][rr, :], [SRCB[0][2 * hp + hh]], [KBO[sl]], group=f"kbo{sl}")
                    dma("sp", vtv[:, hh, 1024:2048], srcs[1][hp][rr, :], [SRCB[1][2 * hp + hh]], [VTO], group="vto")
                for hh in range(2):
                    rr = slice(hh * 128, (hh + 1) * 128)
                    dma("sp", kview[:, hh, 0:1024], gats[0][hp][rr, :], [RELAY[0][hp]], [KBT[sl]], group=f"kbt{sl}")
                    dma("sp", vtv[:, hh, 0:1024], gats[1][hp][rr, :], [RELAY[1][hp]], [VTT], group="vtt")

            def transpose_pair(hp):
                kview, vview = pair_views(hp)
                sl = hp % 2
                for t4 in (2, 3, 0, 1):
                    src_b, dst_b = (VTO, VBO[sl]) if t4 >= 2 else (VTT, VBT[sl])
                    for hh in range(2):
                        bk = ps_alloc()
                        pv = banks[bk][:].bitcast(BF16)
                        for tt in range(4):
                            t = t4 * 4 + tt
                            S.add("pe", lambda hh=hh, t=t, tt=tt, pv=pv: nc.tensor.transpose(
                                out=pv[:, tt * 128:(tt + 1) * 128], in_=vtv[:, hh, t * 128:(t + 1) * 128], identity=ident[:]),
                                [src_b, K], [bank_buf[bk]])
                        vec(lambda hh=hh, t4=t4, pv=pv, vview=vview: nc.vector.tensor_copy(
                            out=vview[:, hh, t4 * 4:(t4 + 1) * 4, :], in_=pv[:, 0:512].rearrange("p (t d) -> p t d", t=4)),
                            [bank_buf[bk]], [dst_b])

            sbufs = [[KB[0], KB[1]], [VB[0], VB[1]]]
            SBb = [[KBb[0], KBb[1]], [VBb[0], VBb[1]]]

            def cache_views(sbi, k_):
                t_ = sbufs[sbi][k_ % 2]
                kv_ = t_[:, 0:2048].rearrange("p (h k) -> p h k", h=NH)
                vv_ = t_[:, 2048:4096].rearrange("p (t d) -> p t d", t=2)
                return kv_, vv_, SBb[sbi][k_ % 2]

            def issue_panel(k_):
                pn = 7 - k_
                for sbi in range(2):
                    kv_, vv_, bb = cache_views(sbi, k_)
                    dma("pool", kv_, ckT[sbi, pn, :, :].rearrange("d (h k) -> d h k", h=NH), [], bb,
                        group=f"ck{sbi}{k_ % 2}")
                    dma("pool", vv_, cv[sbi, pn * 256:(pn + 1) * 256, :].rearrange("(t p) d -> p t d", p=128), [], bb,
                        group=f"ck{sbi}{k_ % 2}")

            noop = lambda: None

            def pseudo(f0=noop, f4=noop):
                return {"st": [f0, noop, noop, noop, f4, noop], "ctx": {}}

            bCs = [ps_alloc(hold=True) for _ in range(2)]
            bOs = [ps_alloc(hold=True) for _ in range(2)]

            def evac_and_clear(prev):
                def f():
                    if prev is not None:
                        php, pqt = prev
                        pq0 = PCOL + pqt * 512
                        for hh in range(2):
                            h = php * 2 + hh
                            vec(lambda h=h, hh=hh, pq0=pq0: nc.vector.tensor_copy(
                                out=oT[:, h, pq0:pq0 + 512], in_=banks[bOs[hh]][:, :]), [bank_buf[bOs[hh]]], [OB[h]])
                    for b in bCs + bOs:
                        vec(lambda b=b: nc.vector.memset(banks[b][:], 0.0), [], [bank_buf[b]])
                return f

            load_pair(0)
            transpose_pair(0)
            all_tasks = []
            prev = None
            for hp in range(4):
                kview, vview = pair_views(hp)
                sl = hp % 2
                for qt in range(2):
                    q0 = PCOL + qt * 512
                    hook0 = noop
                    if qt == 1 and hp + 1 < 4:
                        hook0 = (lambda hp=hp: load_pair(hp + 1))

                    all_tasks.append(pseudo(hook0, evac_and_clear(prev)))
                    all_tasks.append(pseudo())
                    streams = []
                    for hh in range(2):
                        h = hp * 2 + hh
                        tl = []
                        for kb in range(4 * qt + 3, -1, -1):
                            kc0 = 1024 + kb * 128
                            if kb >= 4 * qt:
                                sub = (kb - 4 * qt) * 128
                                ncol = 512 - sub
                                mfn = diag_mask(0)
                            else:
                                sub, ncol, mfn = 0, 512, None
                            qk = [(0, ncol, kview[:, hh, kc0:kc0 + 128], qT[:, h, q0 + sub:q0 + 512], [KBO[sl], QB[h]])]
                            av = [(sub, ncol, vview[:, hh, 8 + kb, :], 0, [VBO[sl]])]
                            tl.append(make_block(qk, av, ncol, 128, zb[:, 0:1], mfn, bCs[hh], bOs[hh], sub, []))
                        for kb in range(7, -1, -1):
                            kc0 = kb * 128
                            qk = [(0, 512, kview[:, hh, kc0:kc0 + 128], qT[:, h, q0:q0 + 512], [KBT[sl], QB[h]])]
                            av = [(0, 512, vview[:, hh, kb, :], 0, [VBT[sl]])]
                            tl.append(make_block(qk, av, 512, 128, flag_t[:, 0:1], None, bCs[hh], bOs[hh], 0, []))
                        streams.append(tl)
                    gtasks = [t for pair in zip(*streams) for t in pair]
                    if qt == 1 and hp + 1 < 4:
                        gtasks.insert(16, pseudo(lambda hp=hp: transpose_pair(hp + 1)))
                    if qt == 0 and hp == 3 and KSTOP > 3:
                        gtasks.insert(16, pseudo(lambda: issue_panel(0)))
                    all_tasks += gtasks
                    prev = (hp, qt)
            run_pipeline(all_tasks)
            evac_and_clear(prev)()
            for b in bCs + bOs:
                ps_release(b)

            if KSTOP <= 3:
                return
            bk = ps_alloc()
            pv = banks[bk][:].bitcast(BF16)
            vstoks = [vstok, vstok2]
            for sbi in range(2):
                if sbi == 1:
                    bk = ps_alloc()
                    pv = banks[bk][:].bitcast(BF16)
                for h in range(NH):
                    S.add("pe", lambda h=h, pv=pv, sbi=sbi: nc.tensor.transpose(
                        out=pv[0:16, h * 128:(h + 1) * 128], in_=vsamp[:, h, sbi * 16:(sbi + 1) * 16], identity=ident[:]),
                        [VS, K], [bank_buf[bk]])
                vec(lambda sbi=sbi, pv=pv: nc.vector.tensor_copy(
                    out=vstoks[sbi][:], in_=pv[0:16, 0:1024].rearrange("p (h d) -> p h d", h=NH)), [bank_buf[bk]], [VSTOK[sbi]])
            bC = ps_alloc(hold=True); bO = ps_alloc(hold=True)
            for b in (bC, bO):
                vec(lambda b=b: nc.vector.memset(banks[b][:], 0.0), [], [bank_buf[b]])

            def run_now(t):
                t["ctx"]["i"] = blk[0] % NB
                blk[0] += 1
                for f in t["st"]:
                    f()
            qcs = [SCOL, SCOL + 16]
            qk = [(sbi * 128 + h * 16, 16, ksamp[:, h, sbi * 16:(sbi + 1) * 16], qT[:, h, qcs[sbi]:qcs[sbi] + 16], [KS, QB[h]])
                  for sbi in range(2) for h in range(NH)]
            av = [(sbi * 128 + h * 16, 16, vstoks[sbi][0:16, h, :], sbi * 128 + h * 16, [VSTOK[sbi]])
                  for sbi in range(2) for h in range(NH)]

            def m16(et, EB):
                for sbi in range(2):
                    vec(lambda sbi=sbi: nc.vector.tensor_tensor(out=et[0:16, sbi * 128:(sbi + 1) * 128],
                                                               in0=et[0:16, sbi * 128:(sbi + 1) * 128], in1=mask8[:, :],
                                                               op=ALU.mult), [EB, K2], [EB])
            run_now(make_block(qk, av, 256, 16, zb[0:16, 0:1], m16, bC, bO, 0, []))
            for k_ in range(8):
                if k_ + 1 < 8:
                    issue_panel(k_ + 1)
                views = [cache_views(sbi, k_) for sbi in range(2)]
                for tl_ in (1, 0):
                    qk = [(sbi * 128 + h * 16, 16, views[sbi][0][:, h, tl_ * 128:(tl_ + 1) * 128],
                           qT[:, h, qcs[sbi]:qcs[sbi] + 16], views[sbi][2] + [QB[h]]) for sbi in range(2) for h in range(NH)]
                    av = [(sbi * 128 + h * 16, 16, views[sbi][1][:, tl_, h * 128:(h + 1) * 128], sbi * 128 + h * 16,
                           views[sbi][2]) for sbi in range(2) for h in range(NH)]
                    run_now(make_block(qk, av, 256, 128, zb[:, 0:1], None, bC, bO, 0, []))
            for sbi in range(2):
                for h in range(NH):
                    vec(lambda sbi=sbi, h=h: nc.vector.tensor_copy(
                        out=oT[:, h, qcs[sbi]:qcs[sbi] + 16], in_=banks[bO][:, sbi * 128 + h * 16: sbi * 128 + (h + 1) * 16]),
                        [bank_buf[bO]], [OB[h]])
            ps_release(bC); ps_release(bO)
            S.barrier()
        pq.close()
        if KSTOP <= 4:
            return

        with ExitStack() as pc:
            xn = sb("xn", [128, 16, NT], BF16, pc)
            XN = [[Buf(f"xn{c}_{g}") for g in range(3)] for c in range(16)]
            ca = sb("ca", [128, 8, NT], BF16, pc)
            CA = [Buf(f"ca{j}") for j in range(8)]
            vec(lambda: nc.vector.memset(ca[:], 0.0), [], CA)
            with ExitStack() as pc1:
                scr = norm_scratch(pc1)
                norm(1, xn, XN, scr)
                UW = NT + 4
                ub = sb("ub", [128, UW], F32, pc1)
                ccs = sb("ccs", [128, NT], F32, pc1)
                cbs = sb("cbs", [128, NT], F32, pc1)
                acc = sb("acc", [128, UW], F32, pc1)
                UB, CCS, CBS, ACC = Buf("ub"), Buf("ccs"), Buf("cbs"), Buf("acc")
                CONVO = Buf("convo")
                pieces = {0: [(0, 353, 0)], 1: [(353, 706, 353)],
                          2: [(706, 1026, 706), (1026, 1042, 1028), (1042, 1058, 1046)]}
                for jp in range(4):
                    pan = []
                    for kind in range(3):
                        pan.append(wload(wpanel(w_in, 0, D, kind * 1024 + jp * 256, 256), 16, 256))
                    for jj in range(2):
                        j = jp * 2 + jj
                        for sbi in range(2):
                            uc = 1026 + sbi * 18
                            S.add("pool", lambda j=j, sbi=sbi, uc=uc: nc.gpsimd.tensor_copy(
                                out=ub[:, uc:uc + 2], in_=state_t[:, j * 4 + sbi * 2: j * 4 + sbi * 2 + 2]), [CST], [UB])
                        for g, (a, b_) in enumerate(CG):
                            n = b_ - a
                            bb = ps_alloc(); bc = ps_alloc(); bx = ps_alloc()
                            for kind, bk in ((0, bb), (1, bc), (2, bx)):
                                wv, wb = pan[kind]
                                for kc in range(16):
                                    mm(banks[bk][:, 0:n], wv[:, kc, jj * 128:(jj + 1) * 128], xn[:, kc, a:b_],
                                       [wb, XN[kc][g]], [bank_buf[bk]], start=(kc == 0), stop=(kc == 15))
                            act(cbs[:, a:b_], banks[bb][:, 0:n], AF.Copy, [bank_buf[bb]], [CBS])
                            act(ccs[:, a:b_], banks[bc][:, 0:n], AF.Copy, [bank_buf[bc]], [CCS])
                            for (ta, tb, uc) in pieces[g]:
                                vec(lambda ta=ta, tb=tb, uc=uc, a=a, bx=bx: nc.vector.tensor_tensor(
                                    out=ub[:, uc:uc + (tb - ta)], in0=ccs[:, ta:tb], in1=banks[bx][:, ta - a:tb - a],
                                    op=ALU.mult), [CCS, bank_buf[bx]], [UB])
                        w0 = convw_t[:, j * 3 + 0:j * 3 + 1]; w1 = convw_t[:, j * 3 + 1:j * 3 + 2]; w2 = convw_t[:, j * 3 + 2:j * 3 + 3]
                        L = UW - 2
                        vec(lambda w2=w2, L=L: nc.vector.tensor_scalar(out=acc[:, 2:UW], in0=ub[:, 2:UW], scalar1=w2, scalar2=None,
                                                                       op0=ALU.mult), [UB, CST], [ACC])
                        vec(lambda w1=w1, L=L: nc.vector.scalar_tensor_tensor(out=acc[:, 2:UW], in0=ub[:, 1:UW - 1], scalar=w1,
                                                                             in1=acc[:, 2:UW], op0=ALU.mult, op1=ALU.add),
                            [UB, ACC, CST], [ACC])
                        vec(lambda w0=w0, L=L: nc.vector.scalar_tensor_tensor(out=acc[:, 2:UW], in0=ub[:, 0:UW - 2], scalar=w0,
                                                                             in1=acc[:, 2:UW], op0=ALU.mult, op1=ALU.add),
                            [UB, ACC, CST], [ACC])
                        for (ta, tb, uc) in ((2, 1026, 2), (1026, 1042, 1028), (1042, 1058, 1046)):
                            vec(lambda j=j, ta=ta, tb=tb, uc=uc: nc.vector.tensor_tensor(
                                out=ca[:, j, ta:tb], in0=cbs[:, ta:tb], in1=acc[:, uc:uc + (tb - ta)], op=ALU.mult),
                                [CBS, ACC], [CA[j]])
                        for k_, uc in enumerate((1024, 1042, 1060)):
                            vec(lambda j=j, k_=k_, uc=uc: nc.vector.tensor_copy(
                                out=convo_t[:, j * 6 + k_ * 2: j * 6 + k_ * 2 + 2], in_=ub[:, uc:uc + 2]), [UB], [CONVO])
                dma("sp", convo, convo_t[:], [CONVO], [], group="convout")
                S.barrier()

            with ExitStack() as pc2:
                mix = [sb(f"mix{i}", [128, 4, NT], BF16, pc2) for i in range(2)]
                MIX = [[[Buf(f"mix{i}_{j}_{g}") for g in range(3)] for j in range(4)] for i in range(2)]
                sga = [sb(f"sga{i}", [128, 353], F32, pc2) for i in range(3)]
                sgb = [sb(f"sgb{i}", [128, 353], F32, pc2) for i in range(3)]
                t1 = [sb(f"t1_{i}", [128, 353], F32, pc2) for i in range(3)]
                t2 = [sb(f"t2_{i}", [128, 353], F32, pc2) for i in range(1)]
                SGA = [Buf(f"sga{i}") for i in range(3)]; SGB = [Buf(f"sgb{i}") for i in range(3)]
                T1 = [Buf(f"t1_{i}") for i in range(3)]; T2 = [Buf("t2_0")]
                cnt = [0]

                def gate_group(g4):
                    s = g4 % 2
                    for j in range(4):
                        m = g4 * 4 + j
                        for kind in range(4):
                            if kind == 0:
                                wv, wb = wload(wpanel(w_in, 0, D, 6144 + m * 128, 128), 16, 128)
                            elif kind == 1:
                                wv, wb = wload(wpanel(w_in, 0, D, 8192 + m * 128, 128), 16, 128)
                            elif kind == 2:
                                wv, wb = wload(wpanel(w_co, 0, 1024, m * 128, 128), 8, 128)
                            else:
                                wv, wb = wload(wpanel(w_ao, 0, 1024, m * 128, 128), 8, 128)
                            nk = 16 if kind < 2 else 8
                            for g, (a, b_) in enumerate(CG):
                                n = b_ - a
                                bk = ps_alloc()
                                for kc in range(nk):
                                    if kind < 2:
                                        rhs, rb = xn[:, kc, a:b_], XN[kc][g]
                                    elif kind == 2:
                                        rhs, rb = ca[:, kc, a:b_], CA[kc]
                                    else:
                                        rhs, rb = oT[:, kc, a:b_], OB[kc]
                                    mm(banks[bk][:, 0:n], wv[:, kc, :], rhs, [wb, rb], [bank_buf[bk]],
                                       start=(kc == 0), stop=(kc == nk - 1))
                                if kind == 0:
                                    act(sga[g][:, 0:n], banks[bk][:, 0:n], AF.Sigmoid, [bank_buf[bk]], [SGA[g]])
                                elif kind == 1:
                                    act(sgb[g][:, 0:n], banks[bk][:, 0:n], AF.Sigmoid, [bank_buf[bk]], [SGB[g]])
                                elif kind == 2:
                                    vec(lambda g=g, n=n, bk=bk: nc.vector.tensor_tensor(
                                        out=t1[g][:, 0:n], in0=sga[g][:, 0:n], in1=banks[bk][:, 0:n], op=ALU.mult),
                                        [SGA[g], bank_buf[bk]], [T1[g]])
                                else:
                                    t = 0
                                    vec(lambda g=g, n=n, bk=bk, t=t: nc.vector.tensor_tensor(
                                        out=t2[t][:, 0:n], in0=sgb[g][:, 0:n], in1=banks[bk][:, 0:n], op=ALU.mult),
                                        [SGB[g], bank_buf[bk]], [T2[t]])
                                    S.add("pool", lambda t=t, n=n, s=s, j=j, a=a, b_=b_, g=g: nc.gpsimd.tensor_tensor(
                                        out=mix[s][:, j, a:b_], in0=t1[g][:, 0:n], in1=t2[t][:, 0:n], op=ALU.add),
                                        [T1[g], T2[t]], [MIX[s][j][g]])

                def wo_group(g4):
                    s = g4 % 2
                    for half in range(2):
                        dv, db_ = wload(wpanel(w_o, g4 * 512, 512, half * 1024, 1024), 4, 1024)
                        for mo in range(8):
                            m = half * 8 + mo
                            for g, (a, b_) in enumerate(CG):
                                n = b_ - a
                                bk = ps_alloc()
                                for kc in range(4):
                                    mm(banks[bk][:, 0:n], dv[:, kc, mo * 128:(mo + 1) * 128], mix[s][:, kc, a:b_],
                                       [db_, MIX[s][kc][g]], [bank_buf[bk]], start=(kc == 0), stop=(kc == 3))
                                vec(lambda m=m, a=a, b_=b_, n=n, bk=bk: nc.vector.tensor_tensor(
                                    out=xres[:, m, a:b_], in0=banks[bk][:, 0:n], in1=xres[:, m, a:b_], op=ALU.add),
                                    [bank_buf[bk], XB[m][g]], [XB[m][g]])

                for g4 in range(4):
                    gate_group(g4)
                    if g4 >= 1:
                        wo_group(g4 - 1)
                wo_group(3)
                S.barrier()

    if KSTOP <= 5:
        return
    with ExitStack() as ph3:
        xn = sb("xn", [128, 16, NT], BF16, ph3)
        XN = [[Buf(f"xn{c}_{g}") for g in range(3)] for c in range(16)]
        scr = norm_scratch(ph3)
        norm(2, xn, XN, scr)
        ffn(xn, XN, w_gu2, w_dn2, ph3)
        yst = [sb(f"yst{i}", [128, 353], F32, ph3) for i in range(4)]
        YST = [Buf(f"yst{i}") for i in range(4)]
        norm(3, None, None, scr, final_out=(yst, YST), per_cg=True)


_CACHE = {}


def _host_inputs(x_prompt, x_sample, cache_k, cache_v, state_conv, norm_ffn1, ffn1_w_gate_up, ffn1_w_down,
                 norm_mix, w_in, conv_w, w_conv_out, w_attn_out, w_o, norm_ffn2, ffn2_w_gate_up, ffn2_w_down,
                 norm_final):
    f = lambda a: np.ascontiguousarray(np.asarray(a, dtype=np.float32))
    x_prompt, x_sample = f(x_prompt), f(x_sample)
    cache_k, cache_v, state_conv = f(cache_k)[0], f(cache_v)[0], f(state_conv)[0]
    gl = np.stack([f(norm_ffn1)[0], f(norm_mix)[0], f(norm_ffn2)[0], f(norm_final)], 0)
    gains = np.ascontiguousarray(gl.reshape(4, 16, 128).transpose(2, 0, 1).reshape(128, 64))
    cw = f(conv_w)[0]
    convw = np.ascontiguousarray(cw.reshape(3, 8, 128).transpose(2, 1, 0).reshape(128, 24))
    shared = {
        "w_gu1": f(ffn1_w_gate_up)[0], "w_dn1": f(ffn1_w_down)[0], "w_in": f(w_in)[0],
        "w_co": f(w_conv_out)[0], "w_ao": f(w_attn_out)[0], "w_o": f(w_o)[0],
        "w_gu2": f(ffn2_w_gate_up)[0], "w_dn2": f(ffn2_w_down)[0], "gains": gains, "convw": convw,
    }
    maps = []
    for c in range(8):
        b, half = c // 2, c % 2
        xt = np.zeros((NT, D), np.float32)
        if half == 1:
            xt[0:2] = x_prompt[b, 1022:1024]
        xt[2:1026] = x_prompt[b, half * 1024:(half + 1) * 1024]
        xt[1026:1042] = x_sample[2 * c]
        xt[1042:1058] = x_sample[2 * c + 1]
        st = state_conv[2 * c:2 * c + 2]
        stT = st.reshape(2, 2, 8, 128).transpose(3, 2, 0, 1).reshape(128, 32)
        m = dict(shared)
        m["xT"] = np.ascontiguousarray(xt.T)
        m["stateT"] = np.ascontiguousarray(stT)
        m["flags"] = np.full((128, 1), 0.0 if half == 1 else NEG, np.float32)
        ck = cache_k[2 * c:2 * c + 2].reshape(2, 8, 256, NH, 128)
        m["ckT"] = np.ascontiguousarray(ck.transpose(0, 1, 4, 3, 2)).reshape(2, 8, 128, NH * 256)
        m["cv"] = np.ascontiguousarray(cache_v[2 * c:2 * c + 2].reshape(2, 2048, 1024))
        maps.append(m)
    return maps


def kernel(**inputs):
    if "nc" not in _CACHE:
        _CACHE["nc"] = build_program()
    nc = _CACHE["nc"]
    maps = _host_inputs(**inputs)
    res = run_bass_kernel_spmd(nc, maps, core_ids=list(range(8)))
    R = res.results
    y_prompt = np.zeros((4, 2048, D), np.float32)
    y_sample = np.zeros((16, 16, D), np.float32)
    k_prompt = np.zeros((1, 4, 2048, NH, 128), np.float32)
    v_prompt = np.zeros((1, 4, 2048, NH, 128), np.float32)
    conv_prompt = np.zeros((1, 4, 2, 1024), np.float32)
    k_sample = np.zeros((1, 16, 16, NH, 128), np.float32)
    v_sample = np.zeros((1, 16, 16, NH, 128), np.float32)
    conv_sample = np.zeros((1, 16, 2, 1024), np.float32)
    for c in range(8):
        b, half = c // 2, c % 2
        yT = R[c]["yT"]; kT = R[c]["kTo"]; vT = R[c]["vTo"]
        cvo = R[c]["convo"].reshape(128, 8, 3, 2)
        sl = slice(half * 1024, (half + 1) * 1024)
        y_prompt[b, sl] = yT[:, 2:1026].T
        k_prompt[0, b, sl] = kT[:, 2:1026].T.reshape(1024, NH, 128)
        v_prompt[0, b, sl] = vT[:, 2:1026].T.reshape(1024, NH, 128)
        cflat = cvo.transpose(2, 3, 1, 0).reshape(3, 2, 1024)
        if half == 1:
            conv_prompt[0, b] = cflat[0]
        for s in range(2):
            sbi = 2 * c + s
            cols = slice(1026 + 16 * s, 1042 + 16 * s)
            y_sample[sbi] = yT[:, cols].T
            k_sample[0, sbi] = kT[:, cols].T.reshape(16, NH, 128)
            v_sample[0, sbi] = vT[:, cols].T.reshape(16, NH, 128)
            conv_sample[0, sbi] = cflat[1 + s]
    return (y_prompt, y_sample, k_prompt, v_prompt, conv_prompt, k_sample, v_sample, conv_sample)
```

```python
import os
import numpy as np
from contextlib import ExitStack
import concourse.bass as bass
import concourse.mybir as mybir
from concourse.bass_utils import run_bass_kernel_spmd

F32 = mybir.dt.float32
BF16 = mybir.dt.bfloat16
AF = mybir.ActivationFunctionType
ALU = mybir.AluOpType

D = 2048
DFF = 5632
DIN = 10240
NH = 8
NT = 1058
CG = [(0, 353), (353, 706), (706, 1058)]
PCOL = 2
SCOL = 1026
EPS = 1e-6
SCALE = 128 ** -0.5
NSLOT = 5
NEG = -30000.0
KSTOP = int(os.environ.get("KSTOP", "99"))
KCORES = int(os.environ.get("KCORES", "8"))


DBG = {}


class _Stop(Exception):
    pass


def _stop(n):
    if KSTOP <= n:
        raise _Stop()


class Buf:
    __slots__ = ("name", "w", "r")

    def __init__(self, name):
        self.name = name
        self.w = None
        self.r = []


class Op:
    __slots__ = ("eng", "fn", "deps", "dma", "group", "gidx", "marked", "rank", "ndep", "cc", "bar", "epoch")

    def __init__(self, eng, fn, dma=False, group=None, cc=False):
        self.eng = eng
        self.fn = fn
        self.deps = []
        self.dma = dma
        self.group = group
        self.gidx = 0
        self.marked = False
        self.rank = 0
        self.ndep = 0
        self.cc = cc
        self.bar = False
        self.epoch = 0


class Sched:
    STRICT = ("act", "dve", "pool")

    def __init__(self, nc, es):
        self.nc = nc
        self.es = es
        self.ops = []
        self.h = {"pe": nc.tensor, "act": nc.scalar, "dve": nc.vector, "pool": nc.gpsimd, "sp": nc.sync}
        self.groups = {}
        self.last = {}
        self.since_barrier = []

    def add(self, eng, fn, reads=(), writes=(), dma=False, group=None, mode="seq", pos=None, cc=False):
        op = Op(eng, fn, dma=dma, group=group, cc=cc)
        deps = []
        for b in reads:
            if b.w is not None:
                deps.append((b.w, True))
        for b in writes:
            if b.w is not None:
                if not (dma and b.w.dma and b not in reads):
                    deps.append((b.w, False))
            for r in b.r:
                deps.append((r, False))
        seen = set()
        for d, raw in deps:
            if d is op or id(d) in seen:
                continue
            if not d.dma and d.eng == eng and not dma:
                if eng not in self.STRICT:
                    continue
            if not d.dma and d.eng == eng and dma:
                continue
            seen.add(id(d))
            op.deps.append(d)
            d.ndep += 1
        if dma:
            for b in list(reads) + list(writes):
                cands = ([b.w] if b.w is not None else []) + (list(b.r) if b in writes else [])
                for d in cands:
                    if d is not None and not d.dma and d.eng == eng and id(d) not in seen:
                        seen.add(id(d))
                        op.deps.append(d)
                        d.ndep += 1
        for b in reads:
            b.r.append(op)
        for b in writes:
            b.w = op
            b.r = []
        if dma or cc:
            g = self.groups.setdefault(group, [mode, 0])
            g[1] += 1
            op.gidx = g[1]
        if pos is None:
            self.ops.append(op)
        else:
            self.ops.insert(pos, op)
        if not dma and not cc:
            self.last[eng] = op
        self.since_barrier.append(op)
        return op

    def index_after(self, op):
        return self.ops.index(op) + 1

    def barrier(self, skip_pool=False):
        lasts = dict(self.last)
        if skip_pool:
            lasts.pop("pool", None)
        lastg = {}
        for o in self.since_barrier:
            if o.dma:
                lastg[o.group] = o
        pend = list(lastg.values())
        for e in ("pe", "act", "dve", "pool", "sp"):
            op = Op(e, None)
            op.bar = True
            for e2, l in lasts.items():
                if e2 != e:
                    op.deps.append(l)
                    l.ndep += 1
            for p in pend:
                op.deps.append(p)
                p.ndep += 1
            self.ops.append(op)
        self.since_barrier = []

    def emit(self):
        nc, es = self.nc, self.es
        ep = 0
        prev_bar = False
        for op in self.ops:
            if prev_bar and not op.bar:
                ep += 1
            op.epoch = ep
            prev_bar = op.bar
        nep = ep + 1
        esem = {(e, k): es.enter_context(nc.semaphore(f"e_{e}_{k}")) for e in self.h for k in range(nep)}
        gsem = {g: es.enter_context(nc.semaphore("g_" + g)) for g in self.groups}
        for op in self.ops:
            for d in op.deps:
                if not d.dma and not d.cc and (d.epoch == op.epoch or d.eng == "pool"):
                    d.marked = True
        cnt = {(e, k): 0 for e in self.h for k in range(nep)}
        for op in self.ops:
            if op.marked:
                cnt[(op.eng, op.epoch)] += 1
                op.rank = cnt[(op.eng, op.epoch)]
        seen = {e: {} for e in self.h}
        for op in self.ops:
            need = {}
            for d in op.deps:
                if d.cc:
                    key, val = ("g", d.group), d.gidx
                elif d.dma:
                    mode, total = self.groups[d.group]
                    key, val = ("g", d.group), 16 * (total if mode == "all" else d.gidx)
                else:
                    if d.epoch != op.epoch and d.eng != "pool":
                        assert d.epoch < op.epoch
                        continue
                    key, val = ("e", (d.eng, d.epoch)), d.rank
                if need.get(key, 0) < val:
                    need[key] = val
            hnd = self.h[op.eng]
            for key, val in need.items():
                if seen[op.eng].get(key, 0) >= val:
                    continue
                seen[op.eng][key] = val
                sem = gsem[key[1]] if key[0] == "g" else esem[key[1]]
                hnd.wait_ge(sem, val)
            if op.fn is None:
                continue
            ins = op.fn()
            if op.cc:
                ins.then_inc(gsem[op.group])
            elif op.dma:
                ins.then_inc(gsem[op.group], 16)
            elif op.marked:
                ins.then_inc(esem[(op.eng, op.epoch)], 1)
        sp = self.h["sp"]
        self.final_counts = dict(cnt)
        for g, (mode, total) in self.groups.items():
            if not g.startswith("cc"):
                sp.wait_ge(gsem[g], 16 * total)
        for (e, k), c in cnt.items():
            if c and k == nep - 1:
                sp.wait_ge(esem[(e, k)], c)


def build_program():
    nc = bass.Bass("TRN2", target_bir_lowering=False)
    es = ExitStack()
    S = Sched(nc, es)

    def din(name, shape, dt=F32):
        return nc.dram_tensor(name, list(shape), dt, kind="ExternalInput").ap()

    def dout(name, shape, dt=F32):
        return nc.dram_tensor(name, list(shape), dt, kind="ExternalOutput").ap()

    xT = din("xT", [D, NT])
    w_gu1 = din("w_gu1", [D, 2 * DFF]); w_dn1 = din("w_dn1", [DFF, D])
    w_in = din("w_in", [D, DIN])
    w_co = din("w_co", [1024, D]); w_ao = din("w_ao", [1024, D]); w_o = din("w_o", [D, D])
    w_gu2 = din("w_gu2", [D, 2 * DFF]); w_dn2 = din("w_dn2", [DFF, D])
    gains = din("gains", [128, 4 * 16])
    convw = din("convw", [128, 8 * 3])
    stateT = din("stateT", [128, 8 * 4])
    flags = din("flags", [128, 1])
    ckT = din("ckT", [2, 8, 128, NH * 256])
    cv = din("cv", [2, 2048, 1024])
    yT = dout("yT", [D, NT]); kTo = dout("kTo", [1024, NT]); vTo = dout("vTo", [1024, NT])
    convo = dout("convo", [128, 8 * 6])
    srcs = [[nc.dram_tensor(f"kv_src{kv}_{hp}", [256, 1024], BF16) for hp in range(4)] for kv in range(2)]
    gats = [[nc.dram_tensor(f"kv_gat{kv}_{hp}", [512, 1024], BF16) for hp in range(4)] for kv in range(2)]

    uniq = [0]

    def sb(name, shape, dt, ctx=None):
        uniq[0] += 1
        DBG[name] = f"{name}_{uniq[0]}"
        return (ctx or es).enter_context(nc.sbuf_tensor(f"{name}_{uniq[0]}", list(shape), dt))

    xres = sb("xres", [128, 16, NT], F32)
    ring = [sb(f"ring{i}", [128, 4096], BF16) for i in range(NSLOT)]
    gain_t = sb("gain_t", [128, 64], F32)
    convw_t = sb("convw_t", [128, 24], F32)
    state_t = sb("state_t", [128, 32], F32)
    flag_t = sb("flag_t", [128, 1], F32)
    iota_t = sb("iota_t", [128, 128], F32)
    tri = sb("tri", [128, 128], BF16)
    utri = sb("utri", [128, 128], BF16)
    ident = sb("ident", [128, 128], BF16)
    maskf = sb("maskf", [128, 128], F32)
    mask8 = sb("mask8", [16, 128], F32)
    ones_b = sb("ones_b", [128, 128], BF16)
    convo_t = sb("convo_t", [128, 48], F32)
    relay_t = sb("relay_t", [128, 8], F32)

    banks = [es.enter_context(nc.psum_tensor(f"bank{i}", [128, 512], F32)) for i in range(8)]
    bank_buf = [Buf(f"bank{i}") for i in range(8)]
    held = set()
    ps_state = {"i": 0}

    def ps_alloc(hold=False):
        for _ in range(16):
            i = ps_state["i"] % 8
            ps_state["i"] += 1
            if i not in held:
                if hold:
                    held.add(i)
                return i
        raise RuntimeError("no psum bank")

    def ps_release(i):
        held.discard(i)

    def mm(out, lhsT, rhs, reads, writes, start=True, stop=True, skip=False):
        if skip:
            return S.add("pe", lambda: nc.tensor.matmul(out, lhsT, rhs, start=start, stop=stop, skip_group_check=True),
                         reads, writes)
        return S.add("pe", lambda: nc.tensor.matmul(out, lhsT, rhs, start=start, stop=stop), reads, writes)

    def act(out, in_, func, reads, writes, **kw):
        return S.add("act", lambda: nc.scalar.activation(out=out, in_=in_, func=func, **kw), reads, writes)

    def vec(fn, reads, writes, eng="dve"):
        return S.add(eng, fn, reads, writes)

    def dma(eng, out, in_, reads, writes, group, mode="seq", pos=None):
        h = S.h[eng]
        return S.add(eng, lambda: h.dma_start(out=out, in_=in_), reads, writes, dma=True, group=group, mode=mode, pos=pos)

    slot_buf = [Buf(f"slot{i}") for i in range(NSLOT)]
    wst = {"n": 0}

    def wload(dram_view, kc, cols):
        i = wst["n"]
        wst["n"] += 1
        s = i % NSLOT
        b = slot_buf[s]
        view = ring[s][:, 0:kc * cols].rearrange("p (k c) -> p k c", k=kc)
        pos = None
        if b.r:
            pos = S.index_after(b.r[-1])
        elif b.w is not None:
            pos = S.index_after(b.w)
        rd = []
        if pos is None and wst.get("early") is not None:
            pos = wst["early"]
            wst["early"] += 1
            if i >= 2:
                rd = [XB[0][0]]
        dma("pool", view, dram_view, rd, [b], group=f"slot{s}", pos=pos)
        return view, b

    def wpanel(w, r0, nr, c0, ncol):
        return w[r0:r0 + nr, c0:c0 + ncol].rearrange("(k p) c -> p k c", p=128)

    CST = Buf("const")
    for t, d_ in ((gain_t, gains), (convw_t, convw), (state_t, stateT), (flag_t, flags)):
        dma("sp", t[:], d_, [], [CST], group="const", mode="all")
    XB = [[Buf(f"x{c}_{g}") for g in range(3)] for c in range(16)]
    for g, (a, b_) in enumerate(CG):
        dma("sp", xres[:, :, a:b_], xT[:, a:b_].rearrange("(c p) n -> p c n", p=128), [XB[0][0]] if g > 0 else [],
            [XB[c][g] for c in range(16)], group=f"xload{g}")
    wst["early"] = len(S.ops)
    IOT = Buf("iota")
    S.add("pool", lambda: nc.gpsimd.iota(iota_t[:], pattern=[[1, 128]], base=0, channel_multiplier=-1,
                                         allow_small_or_imprecise_dtypes=True), [], [IOT])
    K = Buf("konst")
    vec(lambda: nc.vector.tensor_single_scalar(out=tri[:], in_=iota_t[:], scalar=0.0, op=ALU.is_le), [IOT], [K])
    vec(lambda: nc.vector.tensor_single_scalar(out=utri[:], in_=iota_t[:], scalar=0.0, op=ALU.is_gt), [IOT], [K])
    vec(lambda: nc.vector.tensor_single_scalar(out=ident[:], in_=iota_t[:], scalar=0.0, op=ALU.is_equal), [IOT], [K])
    vec(lambda: nc.vector.tensor_single_scalar(out=maskf[:], in_=iota_t[:], scalar=0.0, op=ALU.is_gt), [IOT], [K])
    K2 = Buf("konst2")
    for h in range(NH):
        vec(lambda h=h: nc.vector.tensor_single_scalar(out=mask8[:, h * 16:(h + 1) * 16], in_=iota_t[0:16, 0:16],
                                                       scalar=0.0, op=ALU.is_gt), [IOT], [K2])
    vec(lambda: nc.vector.memset(ones_b[:], 1.0), [], [K])

    def norm(gidx, xn, XN, scr, final_out=None, per_cg=False):
        sq, rs, rstd = scr["sq"], scr["rs"], scr["rstd"]
        SQ, RS, RSTD = scr["SQ"], scr["RS"], scr["RSTD"]
        bks = [ps_alloc(hold=True) for _ in range(3)]
        kq = [0]

        def squares(g):
            a, b_ = CG[g]; n = b_ - a
            for c in range(16):
                s = kq[0] % len(sq); kq[0] += 1
                act(sq[s][:, 0:n], xres[:, c, a:b_], AF.Square, [XB[c][g]], [SQ[s]])
                mm(banks[bks[g]][:, 0:n], ones_b[:], sq[s][:, 0:n], [SQ[s], K], [bank_buf[bks[g]]], start=(c == 0), stop=(c == 15))

        def stats(g):
            a, b_ = CG[g]; n = b_ - a
            act(rs[g][:, 0:n], banks[bks[g]][:, 0:n], AF.Sqrt, [bank_buf[bks[g]], scr["EPSB"]], [RS[g]], scale=1.0 / D,
                bias=scr["eps"][:, 0:1])
            vec(lambda n=n, g=g: nc.vector.reciprocal(out=rstd[g][:, 0:n], in_=rs[g][:, 0:n]), [RS[g]], [RSTD[g]])

        def scaling(g):
            a, b_ = CG[g]; n = b_ - a
            for c in range(16):
                gcol = gain_t[:, gidx * 16 + c: gidx * 16 + c + 1]
                st, ST = final_out if final_out is not None else (None, None)
                if final_out is not None:
                    s_ = (g * 16 + c) % len(st)
                    o_ap, OBUF = st[s_][:, 0:n], ST[s_]
                else:
                    o_ap, OBUF = xn[:, c, a:b_], XN[c][g]
                if c % 3 != 2:
                    vec(lambda c=c, a=a, b_=b_, n=n, g=g, gcol=gcol, o_ap=o_ap: nc.vector.scalar_tensor_tensor(
                        out=o_ap, in0=xres[:, c, a:b_], scalar=gcol, in1=rstd[g][:, 0:n],
                        op0=ALU.mult, op1=ALU.mult), [XB[c][g], RSTD[g], CST], [OBUF])
                else:
                    t_ = scr["nk"][0] % 2; scr["nk"][0] += 1
                    act(scr["nt"][t_][:, 0:n], xres[:, c, a:b_], AF.Copy, [XB[c][g], CST], [scr["NT"][t_]], scale=gcol)
                    S.add("pool", lambda t_=t_, n=n, g=g, o_ap=o_ap: nc.gpsimd.tensor_tensor(
                        out=o_ap, in0=scr["nt"][t_][:, 0:n], in1=rstd[g][:, 0:n], op=ALU.mult),
                        [scr["NT"][t_], RSTD[g]], [OBUF])
                if final_out is not None:
                    dma("sp", yT[c * 128:(c + 1) * 128, a:b_], o_ap, [OBUF], [], group=f"yout{s_}")

        if per_cg:
            squares(0)
            for g in range(3):
                stats(g)
                if g + 1 < 3:
                    squares(g + 1)
                scaling(g)
        else:
            for g in range(3):
                squares(g)
            for g in range(3):
                stats(g)
            for g in range(3):
                scaling(g)
        for b in bks:
            ps_release(b)

    def norm_scratch(ctx):
        scr = {
            "sq": [sb(f"sq{i}", [128, 353], BF16, ctx) for i in range(4)],
            "rs": [sb(f"rs{i}", [128, 353], F32, ctx) for i in range(3)],
            "rstd": [sb(f"rstd{i}", [128, 353], F32, ctx) for i in range(3)],
            "eps": sb("epsc", [128, 1], F32, ctx),
            "nt": [sb(f"nt{i}", [128, 353], F32, ctx) for i in range(2)], "NT": [Buf("nt0"), Buf("nt1")], "nk": [0],
            "SQ": [Buf(f"sq{i}") for i in range(4)], "RS": [Buf(f"rs{i}") for i in range(3)], "RSTD": [Buf(f"rstd{i}") for i in range(3)],
        }
        scr["EPSB"] = Buf("epsb")
        vec(lambda: nc.vector.memset(scr["eps"][:], EPS), [], [scr["EPSB"]])
        return scr

    def ffn(xn, XN, w_gu, w_dn, ctx):
        hb = [sb(f"hb{i}", [128, 4, NT], BF16, ctx) for i in range(2)]
        HB = [[[Buf(f"hb{i}_{j}_{g}") for g in range(3)] for j in range(4)] for i in range(2)]
        tmp = [sb(f"stmp{i}", [128, 353], F32, ctx) for i in range(2)]
        TMP = [Buf("stmp0"), Buf("stmp1")]
        tcount = [0]
        NG = DFF // 512

        def phaseA(gi):
            s = gi % 2
            for half in range(2):
                c0 = gi * 512 + half * 256
                gv, gb_ = wload(wpanel(w_gu, 0, D, c0, 256), 16, 256)
                uv, ub_ = wload(wpanel(w_gu, 0, D, DFF + c0, 256), 16, 256)
                for jj in range(2):
                    j = half * 2 + jj
                    for g, (a, b_) in enumerate(CG):
                        n = b_ - a
                        bg = ps_alloc(); bu = ps_alloc()
                        for kc in range(16):
                            mm(banks[bg][:, 0:n], gv[:, kc, jj * 128:(jj + 1) * 128], xn[:, kc, a:b_],
                               [gb_, XN[kc][g]], [bank_buf[bg]], start=(kc == 0), stop=(kc == 15))
                        for kc in range(16):
                            mm(banks[bu][:, 0:n], uv[:, kc, jj * 128:(jj + 1) * 128], xn[:, kc, a:b_],
                               [ub_, XN[kc][g]], [bank_buf[bu]], start=(kc == 0), stop=(kc == 15))
                        t = tcount[0] % 2; tcount[0] += 1
                        act(tmp[t][:, 0:n], banks[bg][:, 0:n], AF.Silu, [bank_buf[bg]], [TMP[t]])
                        vec(lambda s=s, j=j, a=a, b_=b_, n=n, t=t, bu=bu: nc.vector.tensor_tensor(
                            out=hb[s][:, j, a:b_], in0=tmp[t][:, 0:n], in1=banks[bu][:, 0:n], op=ALU.mult),
                            [TMP[t], bank_buf[bu]], [HB[s][j][g]])

        def phaseB(gi):
            s = gi % 2
            for half in range(2):
                dv, db_ = wload(wpanel(w_dn, gi * 512, 512, half * 1024, 1024), 4, 1024)
                for mo in range(8):
                    m = half * 8 + mo
                    for g, (a, b_) in enumerate(CG):
                        n = b_ - a
                        bk = ps_alloc()
                        for kc in range(4):
                            mm(banks[bk][:, 0:n], dv[:, kc, mo * 128:(mo + 1) * 128], hb[s][:, kc, a:b_],
                               [db_, HB[s][kc][g]], [bank_buf[bk]], start=(kc == 0), stop=(kc == 3))
                        vec(lambda m=m, a=a, b_=b_, n=n, bk=bk: nc.vector.scalar_tensor_tensor(
                            out=xres[:, m, a:b_], in0=banks[bk][:, 0:n], scalar=0.5, in1=xres[:, m, a:b_],
                            op0=ALU.mult, op1=ALU.add), [bank_buf[bk], XB[m][g]], [XB[m][g]])

        for gi in range(NG):
            phaseA(gi)
            if gi >= 1:
                phaseB(gi - 1)
        phaseB(NG - 1)

    try:
        _build_phases(locals())
    except _Stop:
        pass
    S.emit()
    if os.environ.get("KDEBUG"):
        print("marked counts", S.final_counts, "groups", {g: v[1] for g, v in S.groups.items()})
    es.close()
    return nc


def _build_phases(L):
    globals_ = globals()
    for k_, v_ in L.items():
        globals_[k_] = v_
    with ExitStack() as ph:
        xn = sb("xn", [128, 16, NT], BF16, ph)
        XN = [[Buf(f"xn{c}_{g}") for g in range(3)] for c in range(16)]
        scr = norm_scratch(ph)
        norm(0, xn, XN, scr, per_cg=True)
        if KSTOP <= 0:
            return
        ffn(xn, XN, w_gu1, w_dn1, ph)
        S.barrier()
    if KSTOP <= 1:
        return

    with ExitStack() as ph2:
        oT = sb("oT", [128, NH, NT], BF16, ph2)
        ksamp = sb("ksamp", [128, NH, 32], BF16, ph2)
        vsamp = sb("vsamp", [128, NH, 32], BF16, ph2)
        pq = ph2.enter_context(ExitStack())
        qT = sb("qT", [128, NH, NT], BF16, pq)
        QB = [Buf(f"q{h}") for h in range(NH)]
        OB = [Buf(f"o{h}") for h in range(NH)]
        KS = Buf("ksamp"); VS = Buf("vsamp")
        vec(lambda: nc.vector.memset(oT[:], 0.0), [], OB)

        SRCS = [Buf("kvsrc0"), Buf("kvsrc1")]
        GATS = [[Buf(f"gat{kv}_{hp}") for hp in range(4)] for kv in range(2)]
        RELAY = [[Buf(f"relay{kv}_{hp}") for hp in range(4)] for kv in range(2)]
        SRCB = [[Buf(f"srcb{kv}_{h}") for h in range(NH)] for kv in range(2)]
        with ExitStack() as pa:
            xn = sb("xn", [128, 16, NT], BF16, pa)
            XN = [[Buf(f"xn{c}_{g}") for g in range(3)] for c in range(16)]
            scr = norm_scratch(pa)
            norm(1, xn, XN, scr)
            st32 = [sb(f"st32_{i}", [128, NT], F32, pa) for i in range(2)]
            st16 = [sb(f"st16_{i}", [128, NT], BF16, pa) for i in range(2)]
            ST32 = [Buf("st32_0"), Buf("st32_1")]; ST16 = [Buf("st16_0"), Buf("st16_1")]
            sc = [0]
            for kind in range(int(os.environ.get("KKINDS", "3"))):
                for pn in range(4):
                    c0 = 3072 + kind * 1024 + pn * 256
                    wv, wb = wload(wpanel(w_in, 0, D, c0, 256), 16, 256)
                    for jj in range(2):
                        h = pn * 2 + jj
                        s = sc[0] % 2; sc[0] += 1
                        for g, (a, b_) in enumerate(CG):
                            n = b_ - a
                            bk = ps_alloc()
                            for kc in range(16):
                                mm(banks[bk][:, 0:n], wv[:, kc, jj * 128:(jj + 1) * 128], xn[:, kc, a:b_],
                                   [wb, XN[kc][g]], [bank_buf[bk]], start=(kc == 0), stop=(kc == 15))
                            if kind == 0:
                                act(qT[:, h, a:b_], banks[bk][:, 0:n], AF.Copy, [bank_buf[bk]], [QB[h]])
                            else:
                                act(st32[s][:, a:b_], banks[bk][:, 0:n], AF.Copy, [bank_buf[bk]], [ST32[s]])
                                vec(lambda s=s, a=a, b_=b_: nc.vector.tensor_copy(
                                    out=st16[s][:, a:b_], in_=st32[s][:, a:b_]), [ST32[s]], [ST16[s]])
                        if kind > 0:
                            od = kTo if kind == 1 else vTo
                            if not os.environ.get("KNOOUT"):
                                dma("sp", od[h * 128:(h + 1) * 128, :], st32[s][:], [ST32[s]], [], group=f"kvout{s}")
                            r0 = (h % 2) * 128
                            sd = srcs[kind - 1][h // 2]
                            if not os.environ.get("KNOSRC"):
                                dma("sp", sd[r0:r0 + 128, :], st16[s][:, PCOL:PCOL + 1024], [ST16[s]], [SRCB[kind - 1][h]], group=f"kvsrc{s}")
                            dst, DB = (ksamp, KS) if kind == 1 else (vsamp, VS)
                            if not os.environ.get("KNOKS"):
                                vec(lambda s=s, h=h, dst=dst: nc.vector.tensor_copy(out=dst[:, h, :], in_=st16[s][:, SCOL:NT]),
                                    [ST16[s]], [DB])
                            if h % 2 == 1 and not os.environ.get("KNOCC"):
                                kv, hp_ = kind - 1, h // 2
                                S.add("pool", lambda kv=kv, hp_=hp_: nc.gpsimd.collective_compute(
                                    "AllGather", ALU.bypass, replica_groups=[[2 * i, 2 * i + 1] for i in range(KCORES // 2)],
                                    ins=[srcs[kv][hp_].ap().opt()], outs=[gats[kv][hp_].ap().opt()]),
                                    [SRCB[kv][h - 1], SRCB[kv][h]], [GATS[kv][hp_]], group="cc", cc=True)
                                S.add("pool", lambda: nc.gpsimd.memset(relay_t[:], 0.0), [GATS[kv][hp_]], [RELAY[kv][hp_]])
            S.barrier(skip_pool=True)
        if KSTOP <= 2:
            return

        with ExitStack() as pb:
            KB = [sb(f"KB{i}", [128, 4096], BF16, pb) for i in range(2)]
            VB = [sb(f"VB{i}", [128, 4096], BF16, pb) for i in range(2)]
            VT = sb("VTt", [128, 4096], BF16, pb)
            NB = 3
            e_t = [sb(f"e{i}", [128, 512], F32, pb) for i in range(NB)]
            sp_t = [sb(f"sp{i}", [128, 512], BF16, pb) for i in range(NB)]
            w_t = [sb(f"w{i}", [128, 512], F32, pb) for i in range(NB)]
            a_t = [sb(f"a{i}", [128, 512], BF16, pb) for i in range(NB)]
            Eb = [Buf(f"e{i}") for i in range(NB)]; SPb = [Buf(f"sp{i}") for i in range(NB)]
            Wb = [Buf(f"w{i}") for i in range(NB)]; Ab = [Buf(f"a{i}") for i in range(NB)]
            vstok = sb("vstok", [16, NH, 128], BF16, pb)
            vstok2 = sb("vstok2", [16, NH, 128], BF16, pb)
            VSTOK = [Buf("vstok"), Buf("vstok2")]
            zb = sb("zerob", [128, 1], F32, pb)
            ZB = Buf("zb")
            vec(lambda: nc.vector.memset(zb[:], 0.0), [], [ZB])
            blk = [0]

            def run_pipeline(tasks):
                nst = 6
                n = len(tasks)
                for t_ in tasks:
                    t_["ctx"]["i"] = blk[0] % NB
                    blk[0] += 1
                for step in range(n + nst - 1):
                    for s in reversed(range(nst)):
                        t = step - s
                        if 0 <= t < n:
                            tasks[t]["st"][s]()

            def make_block(qk_list, av_list, ncol, kp, bias_ap, mask_fn, bC, bO, c0, extra_reads):
                st = {"i": None}

                def s0():
                    st["bz"] = ps_alloc()
                    bz = st["bz"]
                    for (oc, n_, l, r, rd) in qk_list:
                        mm(banks[bz][0:kp, oc:oc + n_], l, r, rd, [bank_buf[bz]])

                def s1():
                    bz = st["bz"]
                    act(e_t[st["i"]][0:kp, 0:ncol], banks[bz][0:kp, 0:ncol], AF.Exp, [bank_buf[bz], CST, ZB], [Eb[st["i"]]],
                        scale=SCALE, bias=bias_ap)
                    if mask_fn is not None:
                        mask_fn(e_t[st["i"]], Eb[st["i"]])

                def s2():
                    act(sp_t[st["i"]][0:kp, 0:ncol], e_t[st["i"]][0:kp, 0:ncol], AF.Ln, [Eb[st["i"]]], [SPb[st["i"]]], bias=1.0)

                def s3():
                    mm(banks[bC][0:kp, c0:c0 + ncol], tri[0:kp, 0:kp], sp_t[st["i"]][0:kp, 0:ncol], [SPb[st["i"]], K],
                       [bank_buf[bC]], start=False, stop=True, skip=True)

                def s4():
                    act(w_t[st["i"]][0:kp, 0:ncol], banks[bC][0:kp, c0:c0 + ncol], AF.Exp, [bank_buf[bC]], [Wb[st["i"]]], scale=-1.0)
                    vec(lambda: nc.vector.tensor_tensor(out=a_t[st["i"]][0:kp, 0:ncol], in0=e_t[st["i"]][0:kp, 0:ncol],
                                                        in1=w_t[st["i"]][0:kp, 0:ncol], op=ALU.mult), [Eb[st["i"]], Wb[st["i"]]], [Ab[st["i"]]])

                def s5():
                    mm(banks[bC][:, c0:c0 + ncol], utri[0:kp, :], sp_t[st["i"]][0:kp, 0:ncol], [SPb[st["i"]], K],
                       [bank_buf[bC]], start=False, stop=True, skip=True)
                    for (oc, n_, l, ac, rd) in av_list:
                        mm(banks[bO][:, oc:oc + n_], l, a_t[st["i"]][0:kp, ac:ac + n_], [Ab[st["i"]]] + rd + extra_reads,
                           [bank_buf[bO]], start=False, stop=True, skip=True)

                return {"st": [s0, s1, s2, s3, s4, s5], "ctx": st}

            def diag_mask(sub0):
                def f(et, EB):
                    vec(lambda: nc.vector.tensor_tensor(out=et[:, sub0:sub0 + 128], in0=et[:, sub0:sub0 + 128],
                                                        in1=maskf[:], op=ALU.mult), [EB, K], [EB])
                return f

            KBO = [Buf("KBO0"), Buf("KBO1")]; KBT = [Buf("KBT0"), Buf("KBT1")]
            VBO = [Buf("VBO0"), Buf("VBO1")]; VBT = [Buf("VBT0"), Buf("VBT1")]
            VTO = Buf("VTO"); VTT = Buf("VTT")
            KBb = [[KBO[0], KBT[0]], [KBO[1], KBT[1]]]
            VBb = [[VBO[0], VBT[0]], [VBO[1], VBT[1]]]

            def pair_views(hp):
                kb_, vb_ = KB[hp % 2], VB[hp % 2]
                kview = kb_[:].rearrange("p (h k) -> p h k", h=2)
                vview = vb_[:].rearrange("p (h t d) -> p h t d", h=2, t=16)
                return kview, vview

            vtv = VT[:].rearrange("p (h k) -> p h k", h=2)

            def load_pair(hp):
                kview, vview = pair_views(hp)
                sl = hp % 2
                for hh in range(2):
                    rr = slice(hh * 128, (hh + 1) * 128)
                    dma("sp", kview[:, hh, 1024:2048], srcs[0][hp][rr, :], [SRCB[0][2 * hp + hh]], [KBO[sl]], group=f"kbo{sl}")
                    dma("sp", vtv[:, hh, 1024:2048], srcs[1][hp][rr, :], [SRCB[1][2 * hp + hh]], [VTO], group="vto")
                for hh in range(2):
                    rr = slice(hh * 128, (hh + 1) * 128)
                    dma("sp", kview[:, hh, 0:1024], gats[0][hp][rr, :], [RELAY[0][hp]], [KBT[sl]], group=f"kbt{sl}")
                    dma("sp", vtv[:, hh, 0:1024], gats[1][hp][rr, :], [RELAY[1][hp]], [VTT], group="vtt")

            def transpose_pair(hp):
                kview, vview = pair_views(hp)
                sl = hp % 2
                for t4 in (2, 3, 0, 1):
                    src_b, dst_b = (VTO, VBO[sl]) if t4 >= 2 else (VTT, VBT[sl])
                    for hh in range(2):
                        bk = ps_alloc()
                        pv = banks[bk][:].bitcast(BF16)
                        for tt in range(4):
                            t = t4 * 4 + tt
                            S.add("pe", lambda hh=hh, t=t, tt=tt, pv=pv: nc.tensor.transpose(
                                out=pv[:, tt * 128:(tt + 1) * 128], in_=vtv[:, hh, t * 128:(t + 1) * 128], identity=ident[:]),
                                [src_b, K], [bank_buf[bk]])
                        vec(lambda hh=hh, t4=t4, pv=pv, vview=vview: nc.vector.tensor_copy(
                            out=vview[:, hh, t4 * 4:(t4 + 1) * 4, :], in_=pv[:, 0:512].rearrange("p (t d) -> p t d", t=4)),
                            [bank_buf[bk]], [dst_b])

            sbufs = [[KB[0], KB[1]], [VB[0], VB[1]]]
            SBb = [[KBb[0], KBb[1]], [VBb[0], VBb[1]]]

            def cache_views(sbi, k_):
                t_ = sbufs[sbi][k_ % 2]
                kv_ = t_[:, 0:2048].rearrange("p (h k) -> p h k", h=NH)
                vv_ = t_[:, 2048:4096].rearrange("p (t d) -> p t d", t=2)
                return kv_, vv_, SBb[sbi][k_ % 2]

            def issue_panel(k_):
                pn = 7 - k_
                for sbi in range(2):
                    kv_, vv_, bb = cache_views(sbi, k_)
                    dma("pool", kv_, ckT[sbi, pn, :, :].rearrange("d (h k) -> d h k", h=NH), [], bb,
                        group=f"ck{sbi}{k_ % 2}")
                    dma("pool", vv_, cv[sbi, pn * 256:(pn + 1) * 256, :].rearrange("(t p) d -> p t d", p=128), [], bb,
                        group=f"ck{sbi}{k_ % 2}")

            noop = lambda: None

            def pseudo(f0=noop, f4=noop):
                return {"st": [f0, noop, noop, noop, f4, noop], "ctx": {}}

            bCs = [ps_alloc(hold=True) for _ in range(2)]
            bOs = [ps_alloc(hold=True) for _ in range(2)]

            def evac_and_clear(prev):
                def f():
                    if prev is not None:
                        php, pqt = prev
                        pq0 = PCOL + pqt * 512
                        for hh in range(2):
                            h = php * 2 + hh
                            vec(lambda h=h, hh=hh, pq0=pq0: nc.vector.tensor_copy(
                                out=oT[:, h, pq0:pq0 + 512], in_=banks[bOs[hh]][:, :]), [bank_buf[bOs[hh]]], [OB[h]])
                    for b in bCs + bOs:
                        vec(lambda b=b: nc.vector.memset(banks[b][:], 0.0), [], [bank_buf[b]])
                return f

            load_pair(0)
            transpose_pair(0)
            all_tasks = []
            prev = None
            for hp in range(4):
                kview, vview = pair_views(hp)
                sl = hp % 2
                for qt in range(2):
                    q0 = PCOL + qt * 512
                    hook0 = noop
                    if qt == 1 and hp + 1 < 4:
                        hook0 = (lambda hp=hp: load_pair(hp + 1))

                    all_tasks.append(pseudo(hook0, evac_and_clear(prev)))
                    all_tasks.append(pseudo())
                    streams = []
                    for hh in range(2):
                        h = hp * 2 + hh
                        tl = []
                        for kb in range(4 * qt + 3, -1, -1):
                            kc0 = 1024 + kb * 128
                            if kb >= 4 * qt:
                                sub = (kb - 4 * qt) * 128
                                ncol = 512 - sub
                                mfn = diag_mask(0)
                            else:
                                sub, ncol, mfn = 0, 512, None
                            qk = [(0, ncol, kview[:, hh, kc0:kc0 + 128], qT[:, h, q0 + sub:q0 + 512], [KBO[sl], QB[h]])]
                            av = [(sub, ncol, vview[:, hh, 8 + kb, :], 0, [VBO[sl]])]
                            tl.append(make_block(qk, av, ncol, 128, zb[:, 0:1], mfn, bCs[hh], bOs[hh], sub, []))
                        for kb in range(7, -1, -1):
                            kc0 = kb * 128
                            qk = [(0, 512, kview[:, hh, kc0:kc0 + 128], qT[:, h, q0:q0 + 512], [KBT[sl], QB[h]])]
                            av = [(0, 512, vview[:, hh, kb, :], 0, [VBT[sl]])]
                            tl.append(make_block(qk, av, 512, 128, flag_t[:, 0:1], None, bCs[hh], bOs[hh], 0, []))
                        streams.append(tl)
                    gtasks = [t for pair in zip(*streams) for t in pair]
                    if qt == 1 and hp + 1 < 4:
                        gtasks.insert(16, pseudo(lambda hp=hp: transpose_pair(hp + 1)))
                    if qt == 0 and hp == 3 and KSTOP > 3:
                        gtasks.insert(16, pseudo(lambda: issue_panel(0)))
                    all_tasks += gtasks
                    prev = (hp, qt)
            run_pipeline(all_tasks)
            evac_and_clear(prev)()
            for b in bCs + bOs:
                ps_release(b)

            if KSTOP <= 3:
                return
            bk = ps_alloc()
            pv = banks[bk][:].bitcast(BF16)
            vstoks = [vstok, vstok2]
            for sbi in range(2):
                if sbi == 1:
                    bk = ps_alloc()
                    pv = banks[bk][:].bitcast(BF16)
                for h in range(NH):
                    S.add("pe", lambda h=h, pv=pv, sbi=sbi: nc.tensor.transpose(
                        out=pv[0:16, h * 128:(h + 1) * 128], in_=vsamp[:, h, sbi * 16:(sbi + 1) * 16], identity=ident[:]),
                        [VS, K], [bank_buf[bk]])
                vec(lambda sbi=sbi, pv=pv: nc.vector.tensor_copy(
                    out=vstoks[sbi][:], in_=pv[0:16, 0:1024].rearrange("p (h d) -> p h d", h=NH)), [bank_buf[bk]], [VSTOK[sbi]])
            bC = ps_alloc(hold=True); bO = ps_alloc(hold=True)
            for b in (bC, bO):
                vec(lambda b=b: nc.vector.memset(banks[b][:], 0.0), [], [bank_buf[b]])

            def run_now(t):
                t["ctx"]["i"] = blk[0] % NB
                blk[0] += 1
                for f in t["st"]:
                    f()
            qcs = [SCOL, SCOL + 16]
            qk = [(sbi * 128 + h * 16, 16, ksamp[:, h, sbi * 16:(sbi + 1) * 16], qT[:, h, qcs[sbi]:qcs[sbi] + 16], [KS, QB[h]])
                  for sbi in range(2) for h in range(NH)]
            av = [(sbi * 128 + h * 16, 16, vstoks[sbi][0:16, h, :], sbi * 128 + h * 16, [VSTOK[sbi]])
                  for sbi in range(2) for h in range(NH)]

            def m16(et, EB):
                for sbi in range(2):
                    vec(lambda sbi=sbi: nc.vector.tensor_tensor(out=et[0:16, sbi * 128:(sbi + 1) * 128],
                                                               in0=et[0:16, sbi * 128:(sbi + 1) * 128], in1=mask8[:, :],
                                                               op=ALU.mult), [EB, K2], [EB])
            run_now(make_block(qk, av, 256, 16, zb[0:16, 0:1], m16, bC, bO, 0, []))
            for k_ in range(8):
                if k_ + 1 < 8:
                    issue_panel(k_ + 1)
                views = [cache_views(sbi, k_) for sbi in range(2)]
                for tl_ in (1, 0):
                    qk = [(sbi * 128 + h * 16, 16, views[sbi][0][:, h, tl_ * 128:(tl_ + 1) * 128],
                           qT[:, h, qcs[sbi]:qcs[sbi] + 16], views[sbi][2] + [QB[h]]) for sbi in range(2) for h in range(NH)]
                    av = [(sbi * 128 + h * 16, 16, views[sbi][1][:, tl_, h * 128:(h + 1) * 128], sbi * 128 + h * 16,
                           views[sbi][2]) for sbi in range(2) for h in range(NH)]
                    run_now(make_block(qk, av, 256, 128, zb[:, 0:1], None, bC, bO, 0, []))
            for sbi in range(2):
                for h in range(NH):
                    vec(lambda sbi=sbi, h=h: nc.vector.tensor_copy(
                        out=oT[:, h, qcs[sbi]:qcs[sbi] + 16], in_=banks[bO][:, sbi * 128 + h * 16: sbi * 128 + (h + 1) * 16]),
                        [bank_buf[bO]], [OB[h]])
            ps_release(bC); ps_release(bO)
            S.barrier()
        pq.close()
        if KSTOP <= 4:
            return

        with ExitStack() as pc:
            xn = sb("xn", [128, 16, NT], BF16, pc)
            XN = [[Buf(f"xn{c}_{g}") for g in range(3)] for c in range(16)]
            ca = sb("ca", [128, 8, NT], BF16, pc)
            CA = [Buf(f"ca{j}") for j in range(8)]
            vec(lambda: nc.vector.memset(ca[:], 0.0), [], CA)
            with ExitStack() as pc1:
                scr = norm_scratch(pc1)
                norm(1, xn, XN, scr)
                UW = NT + 4
                ub = sb("ub", [128, UW], F32, pc1)
                ccs = sb("ccs", [128, NT], F32, pc1)
                cbs = sb("cbs", [128, NT], F32, pc1)
                acc = sb("acc", [128, UW], F32, pc1)
                UB, CCS, CBS, ACC = Buf("ub"), Buf("ccs"), Buf("cbs"), Buf("acc")
                CONVO = Buf("convo")
                pieces = {0: [(0, 353, 0)], 1: [(353, 706, 353)],
                          2: [(706, 1026, 706), (1026, 1042, 1028), (1042, 1058, 1046)]}
                for jp in range(4):
                    pan = []
                    for kind in range(3):
                        pan.append(wload(wpanel(w_in, 0, D, kind * 1024 + jp * 256, 256), 16, 256))
                    for jj in range(2):
                        j = jp * 2 + jj
                        for sbi in range(2):
                            uc = 1026 + sbi * 18
                            S.add("pool", lambda j=j, sbi=sbi, uc=uc: nc.gpsimd.tensor_copy(
                                out=ub[:, uc:uc + 2], in_=state_t[:, j * 4 + sbi * 2: j * 4 + sbi * 2 + 2]), [CST], [UB])
                        for g, (a, b_) in enumerate(CG):
                            n = b_ - a
                            bb = ps_alloc(); bc = ps_alloc(); bx = ps_alloc()
                            for kind, bk in ((0, bb), (1, bc), (2, bx)):
                                wv, wb = pan[kind]
                                for kc in range(16):
                                    mm(banks[bk][:, 0:n], wv[:, kc, jj * 128:(jj + 1) * 128], xn[:, kc, a:b_],
                                       [wb, XN[kc][g]], [bank_buf[bk]], start=(kc == 0), stop=(kc == 15))
                            act(cbs[:, a:b_], banks[bb][:, 0:n], AF.Copy, [bank_buf[bb]], [CBS])
                            act(ccs[:, a:b_], banks[bc][:, 0:n], AF.Copy, [bank_buf[bc]], [CCS])
                            for (ta, tb, uc) in pieces[g]:
                                vec(lambda ta=ta, tb=tb, uc=uc, a=a, bx=bx: nc.vector.tensor_tensor(
                                    out=ub[:, uc:uc + (tb - ta)], in0=ccs[:, ta:tb], in1=banks[bx][:, ta - a:tb - a],
                                    op=ALU.mult), [CCS, bank_buf[bx]], [UB])
                        w0 = convw_t[:, j * 3 + 0:j * 3 + 1]; w1 = convw_t[:, j * 3 + 1:j * 3 + 2]; w2 = convw_t[:, j * 3 + 2:j * 3 + 3]
                        L = UW - 2
                        vec(lambda w2=w2, L=L: nc.vector.tensor_scalar(out=acc[:, 2:UW], in0=ub[:, 2:UW], scalar1=w2, scalar2=None,
                                                                       op0=ALU.mult), [UB, CST], [ACC])
                        vec(lambda w1=w1, L=L: nc.vector.scalar_tensor_tensor(out=acc[:, 2:UW], in0=ub[:, 1:UW - 1], scalar=w1,
                                                                             in1=acc[:, 2:UW], op0=ALU.mult, op1=ALU.add),
                            [UB, ACC, CST], [ACC])
                        vec(lambda w0=w0, L=L: nc.vector.scalar_tensor_tensor(out=acc[:, 2:UW], in0=ub[:, 0:UW - 2], scalar=w0,
                                                                             in1=acc[:, 2:UW], op0=ALU.mult, op1=ALU.add),
                            [UB, ACC, CST], [ACC])
                        for (ta, tb, uc) in ((2, 1026, 2), (1026, 1042, 1028), (1042, 1058, 1046)):
                            vec(lambda j=j, ta=ta, tb=tb, uc=uc: nc.vector.tensor_tensor(
                                out=ca[:, j, ta:tb], in0=cbs[:, ta:tb], in1=acc[:, uc:uc + (tb - ta)], op=ALU.mult),
                                [CBS, ACC], [CA[j]])
                        for k_, uc in enumerate((1024, 1042, 1060)):
                            vec(lambda j=j, k_=k_, uc=uc: nc.vector.tensor_copy(
                                out=convo_t[:, j * 6 + k_ * 2: j * 6 + k_ * 2 + 2], in_=ub[:, uc:uc + 2]), [UB], [CONVO])
                dma("sp", convo, convo_t[:], [CONVO], [], group="convout")
                S.barrier()

            with ExitStack() as pc2:
                mix = [sb(f"mix{i}", [128, 4, NT], BF16, pc2) for i in range(2)]
                MIX = [[[Buf(f"mix{i}_{j}_{g}") for g in range(3)] for j in range(4)] for i in range(2)]
                sga = [sb(f"sga{i}", [128, 353], F32, pc2) for i in range(3)]
                sgb = [sb(f"sgb{i}", [128, 353], F32, pc2) for i in range(3)]
                t1 = [sb(f"t1_{i}", [128, 353], F32, pc2) for i in range(3)]
                t2 = [sb(f"t2_{i}", [128, 353], F32, pc2) for i in range(1)]
                SGA = [Buf(f"sga{i}") for i in range(3)]; SGB = [Buf(f"sgb{i}") for i in range(3)]
                T1 = [Buf(f"t1_{i}") for i in range(3)]; T2 = [Buf("t2_0")]
                cnt = [0]

                def gate_group(g4):
                    s = g4 % 2
                    for j in range(4):
                        m = g4 * 4 + j
                        for kind in range(4):
                            if kind == 0:
                                wv, wb = wload(wpanel(w_in, 0, D, 6144 + m * 128, 128), 16, 128)
                            elif kind == 1:
                                wv, wb = wload(wpanel(w_in, 0, D, 8192 + m * 128, 128), 16, 128)
                            elif kind == 2:
                                wv, wb = wload(wpanel(w_co, 0, 1024, m * 128, 128), 8, 128)
                            else:
                                wv, wb = wload(wpanel(w_ao, 0, 1024, m * 128, 128), 8, 128)
                            nk = 16 if kind < 2 else 8
                            for g, (a, b_) in enumerate(CG):
                                n = b_ - a
                                bk = ps_alloc()
                                for kc in range(nk):
                                    if kind < 2:
                                        rhs, rb = xn[:, kc, a:b_], XN[kc][g]
                                    elif kind == 2:
                                        rhs, rb = ca[:, kc, a:b_], CA[kc]
                                    else:
                                        rhs, rb = oT[:, kc, a:b_], OB[kc]
                                    mm(banks[bk][:, 0:n], wv[:, kc, :], rhs, [wb, rb], [bank_buf[bk]],
                                       start=(kc == 0), stop=(kc == nk - 1))
                                if kind == 0:
                                    act(sga[g][:, 0:n], banks[bk][:, 0:n], AF.Sigmoid, [bank_buf[bk]], [SGA[g]])
                                elif kind == 1:
                                    act(sgb[g][:, 0:n], banks[bk][:, 0:n], AF.Sigmoid, [bank_buf[bk]], [SGB[g]])
                                elif kind == 2:
                                    vec(lambda g=g, n=n, bk=bk: nc.vector.tensor_tensor(
                                        out=t1[g][:, 0:n], in0=sga[g][:, 0:n], in1=banks[bk][:, 0:n], op=ALU.mult),
                                        [SGA[g], bank_buf[bk]], [T1[g]])
                                else:
                                    t = 0
                                    vec(lambda g=g, n=n, bk=bk, t=t: nc.vector.tensor_tensor(
                                        out=t2[t][:, 0:n], in0=sgb[g][:, 0:n], in1=banks[bk][:, 0:n], op=ALU.mult),
                                        [SGB[g], bank_buf[bk]], [T2[t]])
                                    S.add("pool", lambda t=t, n=n, s=s, j=j, a=a, b_=b_, g=g: nc.gpsimd.tensor_tensor(
                                        out=mix[s][:, j, a:b_], in0=t1[g][:, 0:n], in1=t2[t][:, 0:n], op=ALU.add),
                                        [T1[g], T2[t]], [MIX[s][j][g]])

                def wo_group(g4):
                    s = g4 % 2
                    for half in range(2):
                        dv, db_ = wload(wpanel(w_o, g4 * 512, 512, half * 1024, 1024), 4, 1024)
                        for mo in range(8):
                            m = half * 8 + mo
                            for g, (a, b_) in enumerate(CG):
                                n = b_ - a
                                bk = ps_alloc()
                                for kc in range(4):
                                    mm(banks[bk][:, 0:n], dv[:, kc, mo * 128:(mo + 1) * 128], mix[s][:, kc, a:b_],
                                       [db_, MIX[s][kc][g]], [bank_buf[bk]], start=(kc == 0), stop=(kc == 3))
                                vec(lambda m=m, a=a, b_=b_, n=n, bk=bk: nc.vector.tensor_tensor(
                                    out=xres[:, m, a:b_], in0=banks[bk][:, 0:n], in1=xres[:, m, a:b_], op=ALU.add),
                                    [bank_buf[bk], XB[m][g]], [XB[m][g]])

                for g4 in range(4):
                    gate_group(g4)
                    if g4 >= 1:
                        wo_group(g4 - 1)
                wo_group(3)
                S.barrier()

    if KSTOP <= 5:
        return
    with ExitStack() as ph3:
        xn = sb("xn", [128, 16, NT], BF16, ph3)
        XN = [[Buf(f"xn{c}_{g}") for g in range(3)] for c in range(16)]
        scr = norm_scratch(ph3)
        norm(2, xn, XN, scr)
        ffn(xn, XN, w_gu2, w_dn2, ph3)
        yst = [sb(f"yst{i}", [128, 353], F32, ph3) for i in range(4)]
        YST = [Buf(f"yst{i}") for i in range(4)]
        norm(3, None, None, scr, final_out=(yst, YST), per_cg=True)


_CACHE = {}


def _host_inputs(x_prompt, x_sample, cache_k, cache_v, state_conv, norm_ffn1, ffn1_w_gate_up, ffn1_w_down,
                 norm_mix, w_in, conv_w, w_conv_out, w_attn_out, w_o, norm_ffn2, ffn2_w_gate_up, ffn2_w_down,
                 norm_final):
    f = lambda a: np.ascontiguousarray(np.asarray(a, dtype=np.float32))
    x_prompt, x_sample = f(x_prompt), f(x_sample)
    cache_k, cache_v, state_conv = f(cache_k)[0], f(cache_v)[0], f(state_conv)[0]
    gl = np.stack([f(norm_ffn1)[0], f(norm_mix)[0], f(norm_ffn2)[0], f(norm_final)], 0)
    gains = np.ascontiguousarray(gl.reshape(4, 16, 128).transpose(2, 0, 1).reshape(128, 64))
    cw = f(conv_w)[0]
    convw = np.ascontiguousarray(cw.reshape(3, 8, 128).transpose(2, 1, 0).reshape(128, 24))
    shared = {
        "w_gu1": f(ffn1_w_gate_up)[0], "w_dn1": f(ffn1_w_down)[0], "w_in": f(w_in)[0],
        "w_co": f(w_conv_out)[0], "w_ao": f(w_attn_out)[0], "w_o": f(w_o)[0],
        "w_gu2": f(ffn2_w_gate_up)[0], "w_dn2": f(ffn2_w_down)[0], "gains": gains, "convw": convw,
    }
    maps = []
    for c in range(8):
        b, half = c // 2, c % 2
        xt = np.zeros((NT, D), np.float32)
        if half == 1:
            xt[0:2] = x_prompt[b, 1022:1024]
        xt[2:1026] = x_prompt[b, half * 1024:(half + 1) * 1024]
        xt[1026:1042] = x_sample[2 * c]
        xt[1042:1058] = x_sample[2 * c + 1]
        st = state_conv[2 * c:2 * c + 2]
        stT = st.reshape(2, 2, 8, 128).transpose(3, 2, 0, 1).reshape(128, 32)
        m = dict(shared)
        m["xT"] = np.ascontiguousarray(xt.T)
        m["stateT"] = np.ascontiguousarray(stT)
        m["flags"] = np.full((128, 1), 0.0 if half == 1 else NEG, np.float32)
        ck = cache_k[2 * c:2 * c + 2].reshape(2, 8, 256, NH, 128)
        m["ckT"] = np.ascontiguousarray(ck.transpose(0, 1, 4, 3, 2)).reshape(2, 8, 128, NH * 256)
        m["cv"] = np.ascontiguousarray(cache_v[2 * c:2 * c + 2].reshape(2, 2048, 1024))
        maps.append(m)
    return maps


def kernel(**inputs):
    if "nc" not in _CACHE:
        _CACHE["nc"] = build_program()
    nc = _CACHE["nc"]
    maps = _host_inputs(**inputs)
    res = run_bass_kernel_spmd(nc, maps, core_ids=list(range(8)))
    R = res.results
    y_prompt = np.zeros((4, 2048, D), np.float32)
    y_sample = np.zeros((16, 16, D), np.float32)
    k_prompt = np.zeros((1, 4, 2048, NH, 128), np.float32)
    v_prompt = np.zeros((1, 4, 2048, NH, 128), np.float32)
    conv_prompt = np.zeros((1, 4, 2, 1024), np.float32)
    k_sample = np.zeros((1, 16, 16, NH, 128), np.float32)
    v_sample = np.zeros((1, 16, 16, NH, 128), np.float32)
    conv_sample = np.zeros((1, 16, 2, 1024), np.float32)
    for c in range(8):
        b, half = c // 2, c % 2
        yT = R[c]["yT"]; kT = R[c]["kTo"]; vT = R[c]["vTo"]
        cvo = R[c]["convo"].reshape(128, 8, 3, 2)
        sl = slice(half * 1024, (half + 1) * 1024)
        y_prompt[b, sl] = yT[:, 2:1026].T
        k_prompt[0, b, sl] = kT[:, 2:1026].T.reshape(1024, NH, 128)
        v_prompt[0, b, sl] = vT[:, 2:1026].T.reshape(1024, NH, 128)
        cflat = cvo.transpose(2, 3, 1, 0).reshape(3, 2, 1024)
        if half == 1:
            conv_prompt[0, b] = cflat[0]
        for s in range(2):
            sbi = 2 * c + s
            cols = slice(1026 + 16 * s, 1042 + 16 * s)
            y_sample[sbi] = yT[:, cols].T
            k_sample[0, sbi] = kT[:, cols].T.reshape(16, NH, 128)
            v_sample[0, sbi] = vT[:, cols].T.reshape(16, NH, 128)
            conv_sample[0, sbi] = cflat[1 + s]
    return (y_prompt, y_sample, k_prompt, v_prompt, conv_prompt, k_sample, v_sample, conv_sample)
```
